# Optimizing a Trainium2 kernel written in Bass

```python
import jax, jax.numpy as jnp
from jax import lax
import numpy as np

D_MODEL = 1024
BATCH = 8
SEQ = 4096
DEPTH = 1

D_PLE = 256
D_RNN = 1024
RNN_BLOCKS = 8
RNN_BW = D_RNN // RNN_BLOCKS
RNN_CONV = 4
LRU_C = 8.0
D_CONV = 1024
SHORT_CONV = 3
N_BRANCH = 2
D_FF = 3 * D_MODEL
FFN_CONV = 3
D_IN = 2 * D_RNN + 3 * D_CONV + N_BRANCH * D_MODEL
EPS = 1e-6

kernel_name = "hybrid_rglru_shortconv_block"


def rmsnorm(u, g):
    uf = u.astype(jnp.float32)
    y = uf * lax.rsqrt(jnp.mean(uf * uf, axis=-1, keepdims=True) + EPS)
    return (y * g.astype(jnp.float32)).astype(u.dtype)


def causal_dwconv(u, w):
    K = w.shape[0]
    S = u.shape[1]
    up = jnp.pad(u, ((0, 0), (K - 1, 0), (0, 0)))
    y = up[:, 0:S] * w[0]
    for k in range(1, K):
        y = y + up[:, k:k + S] * w[k]
    return y


def rg_lru(xc, w_a, b_a, w_x, b_x, lam):
    Bn, S, _ = xc.shape
    xb = xc.reshape(Bn, S, RNN_BLOCKS, RNN_BW)
    r = jax.nn.sigmoid((jnp.einsum('bshi,hij->bshj', xb, w_a).reshape(Bn, S, D_RNN) + b_a).astype(jnp.float32))
    i = jax.nn.sigmoid((jnp.einsum('bshi,hij->bshj', xb, w_x).reshape(Bn, S, D_RNN) + b_x).astype(jnp.float32))
    log_a = -LRU_C * r * jax.nn.softplus(-lam.astype(jnp.float32))
    a = jnp.exp(log_a)
    mult = jnp.sqrt(-jnp.expm1(2.0 * log_a))
    b = mult * (i * xc.astype(jnp.float32))

    def combine(lhs, rhs):
        a1, b1 = lhs
        a2, b2 = rhs
        return a1 * a2, a2 * b1 + b2

    _, h = lax.associative_scan(combine, (a, b), axis=1)
    return h.astype(xc.dtype)


def setup_inputs(seed: int = 0) -> dict:
    key = jax.random.key(seed)
    ks = jax.random.split(key, 24)
    f32 = jnp.float32

    def nrm(k, shape, scale):
        return jax.random.normal(k, shape, f32) * scale

    u = jax.random.uniform(ks[10], (DEPTH, D_RNN), f32, 0.9, 0.999)
    a0 = u ** (1.0 / LRU_C)
    lru_lambda = jnp.log(a0) - jnp.log1p(-a0)
    return {
        "x": nrm(ks[0], (BATCH, SEQ, D_MODEL), 1.0),
        "p": nrm(ks[1], (DEPTH, BATCH, SEQ, D_PLE), 1.0),
        "g_mix": 1.0 + nrm(ks[2], (DEPTH, D_MODEL), 0.02),
        "w_in": nrm(ks[3], (DEPTH, D_MODEL, D_IN), D_MODEL ** -0.5),
        "rnn_conv_w": nrm(ks[4], (DEPTH, RNN_CONV, D_RNN), RNN_CONV ** -0.5),
        "rnn_conv_b": nrm(ks[5], (DEPTH, D_RNN), 0.01),
        "w_rg_a": nrm(ks[6], (DEPTH, RNN_BLOCKS, RNN_BW, RNN_BW), RNN_BW ** -0.5),
        "b_rg_a": nrm(ks[7], (DEPTH, D_RNN), 0.01),
        "w_rg_x": nrm(ks[8], (DEPTH, RNN_BLOCKS, RNN_BW, RNN_BW), RNN_BW ** -0.5),
        "b_rg_x": nrm(ks[9], (DEPTH, D_RNN), 0.01),
        "lru_lambda": lru_lambda,
        "sc_conv_w": nrm(ks[11], (DEPTH, SHORT_CONV, D_CONV), SHORT_CONV ** -0.5),
        "w_proj_a": nrm(ks[12], (DEPTH, D_RNN, D_MODEL), D_RNN ** -0.5),
        "w_proj_b": nrm(ks[13], (DEPTH, D_CONV, D_MODEL), D_CONV ** -0.5),
        "w_out": nrm(ks[14], (DEPTH, D_MODEL, D_MODEL), D_MODEL ** -0.5),
        "g_ffn": 1.0 + nrm(ks[15], (DEPTH, D_MODEL), 0.02),
        "w_up": nrm(ks[16], (DEPTH, D_MODEL, 2 * D_FF), D_MODEL ** -0.5),
        "ffn_conv_w": nrm(ks[17], (DEPTH, FFN_CONV, 2 * D_FF), FFN_CONV ** -0.5),
        "ffn_conv_b": nrm(ks[18], (DEPTH, 2 * D_FF), 0.01),
        "w_down": nrm(ks[19], (DEPTH, D_FF, D_MODEL), D_FF ** -0.5),
        "w_ple_gate": nrm(ks[20], (DEPTH, D_MODEL, D_MODEL), D_MODEL ** -0.5),
        "w_ple_proj": nrm(ks[21], (DEPTH, D_PLE, D_MODEL), D_PLE ** -0.5),
        "g_ple": 1.0 + nrm(ks[22], (DEPTH, D_MODEL), 0.02),
        "g_final": 1.0 + nrm(ks[23], (D_MODEL,), 0.02),
    }


def reference(x, p, g_mix, w_in, rnn_conv_w, rnn_conv_b, w_rg_a, b_rg_a, w_rg_x, b_rg_x,
              lru_lambda, sc_conv_w, w_proj_a, w_proj_b, w_out, g_ffn, w_up, ffn_conv_w,
              ffn_conv_b, w_down, w_ple_gate, w_ple_proj, g_ple, g_final):
    split_idx = (D_RNN, 2 * D_RNN, 2 * D_RNN + D_CONV, 2 * D_RNN + 2 * D_CONV,
                 2 * D_RNN + 3 * D_CONV, 2 * D_RNN + 3 * D_CONV + D_MODEL)
    for l in range(DEPTH):
        h = rmsnorm(x, g_mix[l])
        z = h @ w_in[l]
        xr, gr, cb, cc, cx, ga, gb = jnp.split(z, split_idx, axis=-1)

        xr = causal_dwconv(xr, rnn_conv_w[l]) + rnn_conv_b[l]
        ya = rg_lru(xr, w_rg_a[l], b_rg_a[l], w_rg_x[l], b_rg_x[l], lru_lambda[l]) * jax.nn.gelu(gr)

        yb = cb * causal_dwconv(cc * cx, sc_conv_w[l])

        m = jax.nn.sigmoid(ga) * (ya @ w_proj_a[l]) + jax.nn.sigmoid(gb) * (yb @ w_proj_b[l])
        x = x + m @ w_out[l]

        h = rmsnorm(x, g_ffn[l])
        u = causal_dwconv(h @ w_up[l], ffn_conv_w[l]) + ffn_conv_b[l]
        ug, uv = jnp.split(u, 2, axis=-1)
        x = x + (jax.nn.gelu(ug) * uv) @ w_down[l]

        e = rmsnorm(p[l] @ w_ple_proj[l], g_ple[l])
        x = x + jax.nn.sigmoid(x @ w_ple_gate[l]) * e
    return rmsnorm(x, g_final)
```

```python
import contextlib
import numpy as np
import concourse.bass as bass
import concourse.mybir as mybir
from concourse.bass_utils import run_bass_kernel_spmd

F32 = mybir.dt.float32
BF16 = mybir.dt.bfloat16
AF = mybir.ActivationFunctionType
ALU = mybir.AluOpType

ENG_NAMES = ("pe", "act", "dve", "pool", "sp")
N_CORES = 8
D = 1024
T = 512
NS = 5
EPS = 1e-6


class Sched:
    def __init__(self, nc):
        self.nc = nc
        self.ops = {e: [] for e in ENG_NAMES}
        self.last_w = {}
        self.readers = {}
        self.dma_cnt = {}

    def _deps_for(self, eng, reads, writes):
        deps = []
        for t in reads:
            w = self.last_w.get(t)
            if w is not None:
                deps.append(w)
            if isinstance(t, tuple) and t[0] == "ps":
                deps.extend(r for r in self.readers.get(t, ()) if r[1] != eng)
        for t in writes:
            w = self.last_w.get(t)
            if w is not None:
                deps.append(w)
            deps.extend(self.readers.get(t, ()))
        return deps

    def _commit(self, tok, reads, writes):
        for t in reads:
            self.readers.setdefault(t, []).append(tok)
        for t in writes:
            self.last_w[t] = tok
            self.readers[t] = []

    def op(self, eng, fn, reads=(), writes=()):
        reads = list(reads)
        writes = list(writes)
        deps = self._deps_for(eng, reads, writes)
        idx = len(self.ops[eng])
        tok = ("eng", eng, idx)
        self.ops[eng].append(dict(fn=fn, deps=deps, dma=None, signal=False))
        self._commit(tok, reads, writes)
        return tok

    def dma(self, eng, fn, key, n, reads=(), writes=()):
        reads = list(reads)
        writes = list(writes)
        deps = self._deps_for(eng, reads, writes)
        self.dma_cnt[key] = self.dma_cnt.get(key, 0) + 16 * n
        tok = ("dma", key, self.dma_cnt[key])
        self.ops[eng].append(dict(fn=fn, deps=deps, dma=key, signal=False))
        self._commit(tok, reads, writes)
        return tok

    def emit(self, final_waits=()):
        nc = self.nc
        for e in ENG_NAMES:
            for o in self.ops[e]:
                for d in o["deps"]:
                    if d[0] == "eng":
                        self.ops[d[1]][d[2]]["signal"] = True
        cnt_at = {}
        for e in ENG_NAMES:
            c = 0
            for i, o in enumerate(self.ops[e]):
                if o["signal"]:
                    c += 1
                cnt_at[(e, i)] = c
        with contextlib.ExitStack() as st:
            esem = {e: st.enter_context(nc.semaphore("s_" + e)) for e in ENG_NAMES}
            dsem = {k: st.enter_context(nc.semaphore("d_" + str(k))) for k in self.dma_cnt}
            block = st.enter_context(nc.Block())

            def body(e, eng):
                waited = {}
                for i, o in enumerate(self.ops[e]):
                    need = {}
                    for d in o["deps"]:
                        if d[0] == "eng":
                            s, v = ("e", d[1]), cnt_at[(d[1], d[2])]
                        else:
                            s, v = ("d", d[1]), d[2]
                        if v > need.get(s, 0):
                            need[s] = v
                    for s, v in need.items():
                        if v > waited.get(s, 0):
                            waited[s] = v
                            sem = esem[s[1]] if s[0] == "e" else dsem[s[1]]
                            eng.wait_ge(sem, v)
                    r = o["fn"](eng)
                    if o["dma"] is not None:
                        for ins in r:
                            ins.then_inc(dsem[o["dma"]], 16)
                    elif o["signal"]:
                        r.then_inc(esem[e], 1)
                if e == "sp":
                    for k in final_waits:
                        eng.wait_ge(dsem[k], self.dma_cnt[k])

            @block.tensor
            def _(eng):
                body("pe", eng)

            @block.scalar
            def _(eng):
                body("act", eng)

            @block.vector
            def _(eng):
                body("dve", eng)

            @block.gpsimd
            def _(eng):
                body("pool", eng)

            @block.sync
            def _(eng):
                body("sp", eng)


R_GMIX, R_RCW, R_RCB, R_BRA, R_BRX, R_LAM, R_SCW, R_GFFN, R_FCW, R_FCB, R_GPLE, R_GFIN = (
    0, 8, 40, 48, 56, 64, 72, 96, 104, 248, 296, 304)
N_ROWS = 312
C_HBA, C_HBX, C_CL, C_HCL, C_HGP, C_EPS, C_QUART, C_E, C_T1 = 312, 320, 328, 336, 344, 352, 353, 354, 362
N_PRM = 372
NPOOL = 12
NUB = 2


def build(S_TOK=4096):
    NCH = S_TOK // T
    nc = bass.Bass("TRN2", target_bir_lowering=False)

    def din(name, shape):
        return nc.dram_tensor(name, list(shape), F32, kind="ExternalInput").ap()

    x = din("x", [S_TOK, D])
    p = din("p", [S_TOK, 256])
    g_mix = din("g_mix", [1024])
    w_in = din("w_in", [1024, 7168])
    rnn_conv_w = din("rnn_conv_w", [4, 1024])
    rnn_conv_b = din("rnn_conv_b", [1024])
    w_rg_a = din("w_rg_a", [8, 128, 128])
    b_rg_a = din("b_rg_a", [1024])
    w_rg_x = din("w_rg_x", [8, 128, 128])
    b_rg_x = din("b_rg_x", [1024])
    lru_lambda = din("lru_lambda", [1024])
    sc_conv_w = din("sc_conv_w", [3, 1024])
    w_proj_a = din("w_proj_a", [1024, 1024])
    w_proj_b = din("w_proj_b", [1024, 1024])
    w_out = din("w_out", [1024, 1024])
    g_ffn = din("g_ffn", [1024])
    w_up = din("w_up", [1024, 6144])
    ffn_conv_w = din("ffn_conv_w", [3, 6144])
    ffn_conv_b = din("ffn_conv_b", [6144])
    w_down = din("w_down", [3072, 1024])
    w_ple_gate = din("w_ple_gate", [1024, 1024])
    w_ple_proj = din("w_ple_proj", [256, 1024])
    g_ple = din("g_ple", [1024])
    g_final = din("g_final", [1024])
    ident_d = din("ident", [128, 128])
    out = nc.dram_tensor("out", [S_TOK, D], F32, kind="ExternalOutput").ap()

    tiles = []

    def mk(blocks, k0=0, nk=8, srow=None, nsc=None):
        tc_ = sum(b[2] for b in blocks)
        tiles.append(dict(blocks=blocks, k0=k0, nk=nk, srow=srow, tc=tc_, nsc=(tc_ if nsc is None else nsc)))
        return len(tiles) - 1

    TA = [mk([(w_in, h * 512, 512)], srow=R_GMIX) for h in range(2)]
    TB = [mk([(w_in, 4096 + c * 128, 128), (w_in, 3072 + c * 128, 128), (w_in, 2048 + c * 128, 128)], srow=R_GMIX) for c in range(8)]
    TG = [mk([(w_in, 1024 + h * 512, 512)], srow=R_GMIX) for h in range(2)]
    TC = [mk([(w_in, 5120 + c * 128, 128), (w_in, 6144 + c * 128, 128), (w_proj_a, c * 128, 128), (w_proj_b, c * 128, 128)],
             srow=R_GMIX, nsc=256) for c in range(8)]
    TD = [mk([(w_out, h * 512, 512)]) for h in range(2)]
    TE = [mk([(w_up, (2 * e) * 128, 256), (w_up, 3072 + (2 * e) * 128, 256)], srow=R_GFFN) for e in range(12)]
    TF = [[mk([(w_down, g * 512, 512)], k0=kt * 6, nk=6) for kt in range(4)] for g in range(2)]
    TGt = [mk([(w_ple_gate, h * 512, 512)]) for h in range(2)]
    N_WT = len(tiles)
    wscr = nc.dram_tensor("wscr", [N_WT, 128, 4096], BF16, kind="Internal").ap()

    sb = nc.alloc_sbuf_tensor
    wslot = [sb("wslot%d" % s, [128, 4096], BF16) for s in range(NS)]
    stage = [sb("stage%d" % s, [128, 2048], F32) for s in range(2)]
    rgw = sb("rgw", [128, 16, 128], BF16)
    wpp = sb("wpp", [128, 2, 1024], BF16)
    ident = sb("ident_sb", [128, 128], F32)
    ones = sb("ones_sb", [128, 128], BF16)
    prm_rows = sb("prm_rows", [128, 3, 128], F32)
    prm = sb("prm", [128, N_PRM], F32)
    xs = [sb("xs%d" % i, [128, 1024], F32) for i in range(2)]
    pst = sb("pst", [128, 4, 256], F32)
    pT = sb("pT", [128, 2, 512], BF16)
    xTs = [sb("xT%d" % i, [128, 8, 512], F32) for i in range(2)]
    hT = sb("hT", [128, 8, 512], BF16)
    sq = sb("sq", [128, 8, 512], BF16)
    xcb = sq
    xc = sb("xc", [128, 8, 512], F32)
    big = sb("big", [128, 6144], F32)
    bigb = big[:].bitcast(BF16)
    tails_xr = sb("tails_xr", [128, 8, 3], F32)
    tails_q = sb("tails_q", [128, 8, 2], F32)
    tails_u = sb("tails_u", [128, 48, 2], F32)
    hstate = sb("hstate", [128, 8], F32)
    ub = [sb("ub%d" % i, [128, 516], F32) for i in range(NUB)]
    rt = sb("rt", [128, 512], F32)
    rstd = sb("rstd", [128, 512], F32)
    rstd_e = sb("rstd_e", [128, 512], F32)
    rstd1 = sb("rstd1", [128, 512], F32)
    tpool = [sb("tp%d" % i, [128, 512], F32) for i in range(NPOOL)]
    ps = [nc.alloc_psum_tensor("ps%d" % b, [128, 512], F32) for b in range(8)]

    S = Sched(nc)

    class Ring:
        def __init__(self, idxs):
            self.idxs = list(idxs)
            self.i = 0

        def get(self):
            k = self.idxs[self.i % len(self.idxs)]
            self.i += 1
            return tpool[k], ("P", k)

    bank_ctr = [0]

    def newbank():
        b = bank_ctr[0] % 8
        bank_ctr[0] += 1
        return b

    def bigbf(j):
        return bigb[:, j * 512:(j + 1) * 512]

    def mm_group(bank, pairs, reads):
        def fn(e):
            n = len(pairs)
            for i, (l, r) in enumerate(pairs):
                ins = e.matmul(ps[bank][:], lhsT=l, rhs=r, start=(i == 0), stop=(i == n - 1))
            return ins
        S.op("pe", fn, reads=reads, writes=[("ps", bank)])

    def pcol(c):
        return prm[:, c:c + 1]

    S.dma("sp", lambda e: [e.dma_start(out=ident[:], in_=ident_d)], "c_id", 1, writes=["ident"])
    S.op("pool", lambda e: e.memset(ones[:], 1.0), writes=["ones"])
    S.op("pool", lambda e: e.memset(tails_xr[:], 0.0), writes=[("txr", c) for c in range(8)])
    S.op("pool", lambda e: e.memset(tails_q[:], 0.0), writes=[("tq", c) for c in range(8)])
    S.op("pool", lambda e: e.memset(tails_u[:], 0.0), writes=[("tu", c) for c in range(48)])
    S.op("pool", lambda e: e.memset(hstate[:], 0.0), writes=[("hst", c) for c in range(8)])
    S.op("pool", lambda e: e.memset(prm_rows[:], 0.0), writes=["prm_rows"])
    S.op("pool", lambda e: e.memset(prm[:], 0.0), writes=["prm"])

    plist = [(g_mix, R_GMIX, 8), (rnn_conv_w, R_RCW, 32), (rnn_conv_b, R_RCB, 8), (b_rg_a, R_BRA, 8),
             (b_rg_x, R_BRX, 8), (lru_lambda, R_LAM, 8), (sc_conv_w, R_SCW, 24), (g_ffn, R_GFFN, 8),
             (ffn_conv_w, R_FCW, 144), (ffn_conv_b, R_FCB, 48), (g_ple, R_GPLE, 8), (g_final, R_GFIN, 8)]
    pieces = []
    for ap, r0, n in plist:
        flat = ap if len(ap.shape) == 1 else ap.rearrange("a b -> (a b)")
        rows = flat.rearrange("(r q) -> r q", q=128)
        done = 0
        while done < n:
            r = r0 + done
            take = min(n - done, 128 - (r % 128))
            pieces.append((prm_rows[(r % 128):(r % 128) + take, r // 128, :], rows[done:done + take, :]))
            done += take
    S.dma("sp", lambda e: [e.dma_start(out=o, in_=i) for o, i in pieces], "c_prm", len(pieces), writes=["prm_rows"])

    def prm_tr(e):
        for s_, n in ((0, 128), (1, 128), (2, N_ROWS - 256)):
            ins = e.transpose(out=ps[0][:, s_ * 128:s_ * 128 + n], in_=prm_rows[0:n, s_, :], identity=ident[0:n, 0:n])
        return ins
    bank_ctr[0] = 1
    S.op("pe", prm_tr, reads=["prm_rows", "ident"], writes=[("ps", 0)])
    S.op("dve", lambda e: e.tensor_copy(out=prm[:, 0:N_ROWS], in_=ps[0][:, 0:N_ROWS]), reads=[("ps", 0)], writes=["prm"])

    def sl(c, n=8):
        return prm[:, c:c + n]

    def pop(eng, fn):
        S.op(eng, fn, reads=["prm"], writes=["prm"])
    pop("dve", lambda e: e.tensor_scalar(out=sl(C_HBA), in0=sl(R_BRA), scalar1=0.5, scalar2=None, op0=ALU.mult))
    pop("dve", lambda e: e.tensor_scalar(out=sl(C_HBX), in0=sl(R_BRX), scalar1=0.5, scalar2=None, op0=ALU.mult))
    pop("dve", lambda e: e.tensor_scalar(out=sl(C_HGP), in0=sl(R_GPLE), scalar1=0.5, scalar2=None, op0=ALU.mult))
    pop("dve", lambda e: e.memset(sl(C_EPS, 1), EPS))
    pop("dve", lambda e: e.memset(sl(C_QUART, 1), 0.25))
    pop("act", lambda e: e.activation(out=sl(C_E), in_=sl(R_LAM), func=AF.Exp, scale=-1.0))
    pop("dve", lambda e: e.tensor_scalar(out=sl(C_T1), in0=sl(C_E), scalar1=-0.25, scalar2=1.0 / 3.0, op0=ALU.mult, op1=ALU.add))
    pop("dve", lambda e: e.tensor_tensor(out=sl(C_T1), in0=sl(C_T1), in1=sl(C_E), op=ALU.mult))
    pop("dve", lambda e: e.tensor_scalar(out=sl(C_T1), in0=sl(C_T1), scalar1=-1.0, scalar2=0.5, op0=ALU.mult, op1=ALU.add))
    pop("dve", lambda e: e.tensor_tensor(out=sl(C_T1), in0=sl(C_T1), in1=sl(C_E), op=ALU.mult))
    pop("dve", lambda e: e.tensor_scalar(out=sl(C_T1), in0=sl(C_T1), scalar1=-1.0, scalar2=1.0, op0=ALU.mult, op1=ALU.add))
    pop("dve", lambda e: e.tensor_tensor(out=sl(C_T1), in0=sl(C_T1), in1=sl(C_E), op=ALU.mult))
    pop("dve", lambda e: e.tensor_scalar(out=sl(C_CL), in0=sl(C_T1), scalar1=-8.0, scalar2=None, op0=ALU.mult))
    pop("dve", lambda e: e.tensor_scalar(out=sl(C_HCL), in0=sl(C_T1), scalar1=-4.0, scalar2=None, op0=ALU.mult))

    st0, st1 = stage[0], stage[1]
    S.dma("sp", lambda e: [e.dma_start(out=st0[:, 0:1024].rearrange("q (h j) -> q h j", h=8), in_=w_rg_a.rearrange("h i j -> i h j")),
                           e.dma_start(out=st0[:, 1024:2048].rearrange("q (h j) -> q h j", h=8), in_=w_rg_x.rearrange("h i j -> i h j"))],
          "stg0", 2, writes=[("stage", 0)])
    S.op("dve", lambda e: e.tensor_copy(out=rgw[:].rearrange("q h j -> q (h j)"), in_=st0[:, 0:2048]), reads=[("stage", 0)], writes=["rgw"])
    S.dma("sp", lambda e: [e.dma_start(out=st1[:, 0:2048].rearrange("q (k n) -> q k n", k=2), in_=w_ple_proj.rearrange("(k q) n -> q k n", q=128))],
          "stg1", 1, writes=[("stage", 1)])
    S.op("dve", lambda e: e.tensor_copy(out=wpp[:].rearrange("q k n -> q (k n)"), in_=st1[:, 0:2048]), reads=[("stage", 1)], writes=["wpp"])

    cast_rr = [0]
    xT1flat = xTs[1][:].rearrange("q c t -> q (c t)")
    stage_bufs = [(stage[0][:, :], [("stage", 0)]), (stage[1][:, :], [("stage", 1)]),
                  (xT1flat[:, 0:2048], [("xT1", k) for k in range(4)]), (xT1flat[:, 2048:4096], [("xT1", k) for k in range(4, 8)])]

    def cast_op(en, dst_ap, src_ap, scale_col, reads, writes):
        if en == "act":
            if scale_col is None:
                S.op("act", lambda e: e.activation(out=dst_ap, in_=src_ap, func=AF.Copy), reads=reads, writes=writes)
            else:
                S.op("act", lambda e: e.activation(out=dst_ap, in_=src_ap, func=AF.Identity, scale=pcol(scale_col)), reads=reads + ["prm"], writes=writes)
        else:
            if scale_col is None:
                S.op(en, lambda e: e.tensor_copy(out=dst_ap, in_=src_ap), reads=reads, writes=writes)
            else:
                S.op(en, lambda e: e.tensor_scalar(out=dst_ap, in0=src_ap, scalar1=pcol(scale_col), scalar2=None, op0=ALU.mult), reads=reads + ["prm"], writes=writes)

    stream = TA + TB + TG + TC + TD + TE + TF[0] + TF[1] + TGt
    L = len(stream)
    assert sorted(stream) == list(range(N_WT))
    NSTR = L * NCH
    st_next = [L]
    st_used = [0]
    retired = [False] * NSTR
    cur = {}
    PF = 3

    def stage_of(i, half):
        return half if i == L - 1 else (2 * i + half) % 4

    def c0_load(i):
        td = tiles[stream[i]]
        hk, tc_ = td["nk"] // 2, td["tc"]
        for half in range(2):
            g_ = stage_of(i, half)
            stg, sids = stage_bufs[g_]
            stg3 = stg[:, 0:hk * tc_].rearrange("q (k n) -> q k n", k=hk)
            srcs = []
            off = 0
            for (W, c0, ncol) in td["blocks"]:
                kk0 = td["k0"] + half * hk
                srcs.append((stg3[:, :, off:off + ncol], W.rearrange("(k q) n -> q k n", q=128)[:, kk0:kk0 + hk, c0:c0 + ncol]))
                off += ncol
            S.dma("sp", lambda e, srcs=srcs: [e.dma_start(out=o, in_=i_) for o, i_ in srcs], "stg%d" % g_, len(srcs), writes=sids)

    def c0_cast(i):
        ti = stream[i]
        td = tiles[ti]
        hk, tc_ = td["nk"] // 2, td["tc"]
        s_ = i % NS
        assert i < NS or retired[i - NS], "slot not free for chunk-0 cast of stream idx %d" % i
        cur[ti] = (i, s_)
        slot = wslot[s_]
        wid = ("wslot", s_)
        for half in range(2):
            g_ = stage_of(i, half)
            stg, sids = stage_bufs[g_]
            stg3 = stg[:, 0:hk * tc_].rearrange("q (k n) -> q k n", k=hk)
            dst3 = slot[:, half * hk * tc_:(half + 1) * hk * tc_].rearrange("q (k n) -> q k n", k=hk)
            if td["srow"] is None:
                en = ("act", "dve")[cast_rr[0] % 2]
                cast_rr[0] += 1
                cast_op(en, dst3, stg3, None, list(sids), [wid])
            else:
                en = ("act", "dve")[cast_rr[0] % 2]
                cast_rr[0] += 1
                nsc = td["nsc"]
                for kk in range(hk):
                    cast_op(en, dst3[:, kk, 0:nsc], stg3[:, kk, 0:nsc], td["srow"] + td["k0"] + half * hk + kk, list(sids), [wid])
                if nsc < tc_:
                    cast_op("pool", dst3[:, :, nsc:tc_], stg3[:, :, nsc:tc_], None, list(sids), [wid])

    def c0_store(i):
        if NCH == 1:
            return
        ti = stream[i]
        td = tiles[ti]
        n_ = td["nk"] * td["tc"]
        s_ = i % NS
        S.dma("sp", lambda e: [e.dma_start(out=wscr[ti, :, 0:n_], in_=wslot[s_][:, 0:n_])], "cst%d" % s_, 1,
              reads=[("wslot", s_)], writes=[("wscr", ti)])

    def wload_scratch(i):
        ti = stream[i % L]
        s_ = i % NS
        n_ = tiles[ti]["nk"] * tiles[ti]["tc"]
        S.dma("sp", lambda e: [e.dma_start(out=wslot[s_][:, 0:n_], in_=wscr[ti, :, 0:n_])], "wl%d" % s_, 1,
              reads=[("wscr", ti)], writes=[("wslot", s_)])
        cur[ti] = (i, s_)

    def pump():
        while st_next[0] < NSTR and st_next[0] <= st_used[0] + PF and retired[st_next[0] - NS]:
            wload_scratch(st_next[0])
            st_next[0] += 1

    def wuse(ti):
        i = st_used[0]
        assert stream[i % L] == ti, (i, ti)
        if i < L:
            c0_store(i)
            if i + 1 < L:
                c0_cast(i + 1)
            if i + 2 < L:
                c0_load(i + 2)
        else:
            pump()
            assert st_next[0] > i, "weight ring too small / prefetch stalled at stream idx %d" % i
        st_used[0] += 1
        pump()

    def wdone(ti):
        i, _ = cur[ti]
        retired[i] = True
        pump()

    def wt(ti, k, blk, sub=0):
        td = tiles[ti]
        _, s_ = cur[ti]
        off = sum(b[2] for b in td["blocks"][:blk]) + sub * 128
        base = k * td["tc"] + off
        return wslot[s_][:, base:base + 128], ("wslot", s_)

    def zmm(ti, blk, sub, rhs_fn, rhs_reads, nk=8):
        b = newbank()
        pairs, wid = [], None
        for k in range(nk):
            l, wid = wt(ti, k, blk, sub)
            pairs.append((l, rhs_fn(k)))
        mm_group(b, pairs, reads=list(rhs_reads) + [wid])
        return b

    hTs = [hT, stage[0][:].bitcast(BF16).rearrange("q (c t) -> q c t", c=8)]

    def hctx(n):
        hb = hTs[n % 2]
        HT = "hT%d" % (n % 2)
        return hb, HT, [(HT, k) for k in range(8)], (lambda k: hb[:, k, :])

    ub_i = [0]

    def ub_get():
        k_ = ub_i[0] % NUB
        ub_i[0] += 1
        return ub[k_], ("ub", k_), ("ubh", k_)

    def norm_stats(dst_rstd, dst_id, alt=False):
        b = newbank()
        if alt:
            mm_group(b, [(ones[:], bigbf(8 + k)) for k in range(8)], reads=["ones"] + [("big", 8 + k) for k in range(8)])
        else:
            mm_group(b, [(ones[:], sq[:, k, :]) for k in range(8)], reads=["ones"] + [("sq", k) for k in range(8)])
        S.op("act", lambda e: e.activation(out=rt[:], in_=ps[b][:], func=AF.Ln, scale=1.0 / 1024.0, bias=pcol(C_EPS)),
             reads=[("ps", b), "prm"], writes=["rt"])
        S.op("act", lambda e: e.activation(out=dst_rstd[:], in_=rt[:], func=AF.Exp, scale=-0.5), reads=["rt"], writes=[dst_id])

    def S0a_groups(n):
        return [lambda tt=tt, g=g: S0a_group(n, tt, g) for tt in range(4) for g in range(2)]

    def part_S0a(n):
        for f_ in S0a_groups(n):
            f_()

    def S0a_group(n, tt, g):
        t0 = n * T
        xT = xTs[n % 2]
        XT = "xT%d" % (n % 2)
        if True:
            xb = xs[tt % 2]
            xid = ("xs", tt % 2)
            if g == 0:
                S.dma("sp", lambda e, xb=xb, tt=tt: [e.dma_start(out=xb[:], in_=x[t0 + tt * 128:t0 + (tt + 1) * 128, :])],
                      "xs%d" % (tt % 2), 1, writes=[xid])
            if True:
                b = newbank()

                def trf(e, xb=xb, g=g, b=b):
                    for q in range(4):
                        c = 4 * g + q
                        ins = e.transpose(out=ps[b][:, q * 128:(q + 1) * 128], in_=xb[:, c * 128:(c + 1) * 128], identity=ident[:])
                    return ins
                S.op("pe", trf, reads=[xid, "ident"], writes=[("ps", b)])
                S.op("act", lambda e, g=g, b=b, tt=tt: e.activation(out=xT[:, 4 * g:4 * g + 4, tt * 128:(tt + 1) * 128],
                                                                  in_=ps[b][:].rearrange("q (c t) -> q c t", c=4), func=AF.Copy),
                     reads=[("ps", b)], writes=[(XT, 4 * g + q) for q in range(4)])
                S.op("act", lambda e, g=g, b=b, tt=tt: e.activation(out=sq[:, 4 * g:4 * g + 4, tt * 128:(tt + 1) * 128],
                                                                  in_=ps[b][:].rearrange("q (c t) -> q c t", c=4), func=AF.Square),
                     reads=[("ps", b)], writes=[("sq", 4 * g + q) for q in range(4)])

    def part_S0a2(n):
        norm_stats(rstd1, "rstd1")

    def part_S0b(n):
        xT = xTs[n % 2]
        XT = "xT%d" % (n % 2)
        hT, HT, hT_reads, h_rhs = hctx(n)
        extra = [("stage", 0)] if n == 1 else []
        for k in range(8):
            S.op("dve", lambda e, k=k: e.tensor_tensor(out=hT[:, k, :], in0=xT[:, k, :], in1=rstd1[:], op=ALU.mult),
                 reads=[(XT, k), "rstd1"], writes=[(HT, k)] + extra)

    def loop1_gen(n):
        t0 = n * T
        hT, HT, hT_reads, h_rhs = hctx(n)
        S.dma("sp", lambda e: [e.dma_start(out=pst[:], in_=p[t0:t0 + T, :].rearrange("(tt q) f -> q tt f", q=128))], "pst", 1, writes=["pst"])

        r_tr, r_a, r_hm, r_ti = Ring([0]), Ring([1, 2, 3]), Ring([4, 5, 6]), Ring([7, 8, 9])
        r_cx, r_acc = Ring([10]), Ring([11])
        b1 = {}

        def stage_A(c):
            if c % 4 == 0:
                wuse(TA[c // 4])
            b = zmm(TA[c // 4], 0, c % 4, h_rhs, hT_reads)
            if c % 4 == 3:
                wdone(TA[c // 4])
            u, uid, uhid = ub_get()
            S.op("pool", lambda e: e.tensor_copy(out=u[:, 0:3], in_=tails_xr[:, c, :]), reads=[("txr", c)], writes=[uhid])
            S.op("act", lambda e: e.activation(out=u[:, 3:515], in_=ps[b][:], func=AF.Copy), reads=[("ps", b)], writes=[uid])
            S.op("act", lambda e: e.activation(out=xc[:, c, :], in_=ps[b][:], func=AF.Identity, scale=pcol(R_RCW + 24 + c), bias=pcol(R_RCB + c)),
                 reads=[("ps", b), "prm"], writes=[("xc", c)])
            S.op("pool", lambda e: e.tensor_copy(out=tails_xr[:, c, :], in_=u[:, 512:515]), reads=[uid], writes=[("txr", c)])
            for s_ in (1, 2, 3):
                S.op("dve", lambda e, s_=s_: e.scalar_tensor_tensor(out=xc[:, c, :], in0=u[:, 3 - s_:515 - s_], scalar=pcol(R_RCW + (3 - s_) * 8 + c),
                                                                  in1=xc[:, c, :], op0=ALU.mult, op1=ALU.add),
                     reads=[uid, uhid, ("xc", c), "prm"], writes=[("xc", c)])
            S.op("pool", lambda e: e.tensor_copy(out=xcb[:, c, :], in_=xc[:, c, :]), reads=[("xc", c)], writes=[("sq", c)])

        def stage_B1a(c):
            b_ra = newbank()
            mm_group(b_ra, [(rgw[:, c, :], xcb[:, c, :])], reads=["rgw", ("sq", c)])
            b_rx = newbank()
            mm_group(b_rx, [(rgw[:, 8 + c, :], xcb[:, c, :])], reads=["rgw", ("sq", c)])
            tr, tr_id = r_tr.get()
            a_, a_id = r_a.get()
            hm, hm_id = r_hm.get()
            ti_, ti_id = r_ti.get()
            b1[c] = (a_, a_id, hm, hm_id, ti_, ti_id)
            S.op("act", lambda e: e.activation(out=tr[:], in_=ps[b_ra][:], func=AF.Tanh, scale=0.5, bias=pcol(C_HBA + c)),
                 reads=[("ps", b_ra), "prm"], writes=[tr_id])
            S.op("act", lambda e: e.activation(out=ti_[:], in_=ps[b_rx][:], func=AF.Tanh, scale=0.5, bias=pcol(C_HBX + c)),
                 reads=[("ps", b_rx), "prm"], writes=[ti_id])
            S.op("act", lambda e: e.activation(out=a_[:], in_=tr[:], func=AF.Exp, scale=pcol(C_HCL + c), bias=pcol(C_HCL + c)),
                 reads=[tr_id, "prm"], writes=[a_id])
            S.op("act", lambda e: e.activation(out=hm[:], in_=tr[:], func=AF.Exp, scale=pcol(C_CL + c), bias=pcol(C_CL + c)),
                 reads=[tr_id, "prm"], writes=[hm_id])
            S.op("act", lambda e: e.activation(out=hm[:], in_=hm[:], func=AF.Sqrt, scale=-0.25, bias=pcol(C_QUART)),
                 reads=[hm_id, "prm"], writes=[hm_id])

        def stage_B1b(c):
            a_, a_id, hm, hm_id, ti_, ti_id = b1[c]
            S.op("dve", lambda e: e.scalar_tensor_tensor(out=ti_[:], in0=ti_[:], scalar=1.0, in1=xc[:, c, :], op0=ALU.add, op1=ALU.mult),
                 reads=[ti_id, ("xc", c)], writes=[ti_id])
            S.op("dve", lambda e: e.tensor_tensor(out=ti_[:], in0=ti_[:], in1=hm[:], op=ALU.mult), reads=[ti_id, hm_id], writes=[ti_id])
            S.op("dve", lambda e: e.tensor_tensor_scan(out=xc[:, c, :], data0=a_[:], data1=ti_[:], initial=hstate[:, c:c + 1], op0=ALU.mult, op1=ALU.add),
                 reads=[a_id, ti_id, ("hst", c)], writes=[("xc", c)])
            S.op("pool", lambda e: e.tensor_copy(out=hstate[:, c:c + 1], in_=xc[:, c, 511:512]), reads=[("xc", c)], writes=[("hst", c)])

        def stage_B2(c):
            ti = TB[c]
            wuse(ti)
            b_cx = zmm(ti, 0, 0, h_rhs, hT_reads)
            cxs, cx_id = r_cx.get()
            S.op("act", lambda e: e.activation(out=cxs[:], in_=ps[b_cx][:], func=AF.Copy), reads=[("ps", b_cx)], writes=[cx_id])
            b_cc = zmm(ti, 1, 0, h_rhs, hT_reads)
            u, uid, uhid = ub_get()
            S.op("pool", lambda e: e.tensor_copy(out=u[:, 1:3], in_=tails_q[:, c, :]), reads=[("tq", c)], writes=[uhid])
            S.op("dve", lambda e: e.tensor_tensor(out=u[:, 3:515], in0=ps[b_cc][:], in1=cxs[:], op=ALU.mult),
                 reads=[("ps", b_cc), cx_id], writes=[uid])
            S.op("pool", lambda e: e.tensor_copy(out=tails_q[:, c, :], in_=u[:, 513:515]), reads=[uid], writes=[("tq", c)])
            acc, acc_id = r_acc.get()
            S.op("dve", lambda e: e.tensor_scalar(out=acc[:], in0=u[:, 3:515], scalar1=pcol(R_SCW + 16 + c), scalar2=None, op0=ALU.mult),
                 reads=[uid, "prm"], writes=[acc_id])
            S.op("dve", lambda e: e.scalar_tensor_tensor(out=acc[:], in0=u[:, 2:514], scalar=pcol(R_SCW + 8 + c), in1=acc[:], op0=ALU.mult, op1=ALU.add),
                 reads=[uid, uhid, acc_id, "prm"], writes=[acc_id])
            S.op("dve", lambda e: e.scalar_tensor_tensor(out=acc[:], in0=u[:, 1:513], scalar=pcol(R_SCW + 0 + c), in1=acc[:], op0=ALU.mult, op1=ALU.add),
                 reads=[uid, uhid, acc_id, "prm"], writes=[acc_id])
            b_cb = zmm(ti, 2, 0, h_rhs, hT_reads)
            wdone(ti)
            S.op("dve", lambda e: e.tensor_tensor(out=bigbf(8 + c), in0=ps[b_cb][:], in1=acc[:], op=ALU.mult),
                 reads=[("ps", b_cb), acc_id], writes=[("big", 8 + c)])

        for t_ in range(12):
            if t_ < 8:
                stage_A(t_)
            if t_ >= 4:
                stage_B2(t_ - 4)
            if 2 <= t_ < 10:
                stage_B1a(t_ - 2)
            if 3 <= t_ < 11:
                stage_B1b(t_ - 3)
            yield t_

    def main_rest(n):
        xT = xTs[n % 2]
        XT = "xT%d" % (n % 2)
        s0a_next = S0a_groups(n + 1) if (n + 1 < NCH and n > 0) else None
        hT, HT, hT_reads, h_rhs = hctx(n)
        r_gg = Ring([0, 1, 2])
        for c in range(8):
            if c % 4 == 0:
                wuse(TG[c // 4])
            b_gr = zmm(TG[c // 4], 0, c % 4, h_rhs, hT_reads)
            if c % 4 == 3:
                wdone(TG[c // 4])
            gg, gg_id = r_gg.get()
            S.op("act", lambda e, gg=gg, b=b_gr: e.activation(out=gg[:], in_=ps[b][:], func=AF.Gelu_apprx_tanh), reads=[("ps", b_gr)], writes=[gg_id])
            S.op("dve", lambda e, gg=gg, c=c: e.tensor_tensor(out=bigbf(c), in0=xc[:, c, :], in1=gg[:], op=ALU.mult),
                 reads=[("xc", c), gg_id], writes=[("big", c)])

        ya_reads = [("big", k) for k in range(8)]
        yb_reads = [("big", 8 + k) for k in range(8)]
        r_tg = Ring([3, 4, 5, 6, 7, 8])
        for c in range(8):
            ti = TC[c]
            wuse(ti)
            parts = []
            for (gblk, pblk, yoff, yreads) in ((0, 2, 0, ya_reads), (1, 3, 8, yb_reads)):
                b_g = zmm(ti, gblk, 0, h_rhs, hT_reads)
                tg, tg_id = r_tg.get()
                S.op("act", lambda e, tg=tg, b=b_g: e.activation(out=tg[:], in_=ps[b][:], func=AF.Tanh, scale=0.5), reads=[("ps", b_g)], writes=[tg_id])
                b_p = zmm(ti, pblk, 0, lambda k, yoff=yoff: bigbf(yoff + k), yreads)
                S.op("dve", lambda e, tg=tg, b=b_p: e.scalar_tensor_tensor(out=tg[:], in0=tg[:], scalar=1.0, in1=ps[b][:], op0=ALU.add, op1=ALU.mult),
                     reads=[tg_id, ("ps", b_p)], writes=[tg_id])
                parts.append((tg, tg_id))
            wdone(ti)
            (ma, ma_id), (mb, mb_id) = parts
            S.op("pool", lambda e, ma=ma, mb=mb, c=c: e.tensor_tensor(out=bigbf(16 + c), in0=ma[:], in1=mb[:], op=ALU.add),
                 reads=[ma_id, mb_id], writes=[("big", 16 + c)])

        m_reads = [("big", 16 + k) for k in range(8)]
        for c in range(8):
            if c % 4 == 0:
                wuse(TD[c // 4])
            b = zmm(TD[c // 4], 0, c % 4, lambda k: bigbf(16 + k), m_reads)
            if c % 4 == 3:
                wdone(TD[c // 4])
            S.op("dve", lambda e, b=b, c=c: e.scalar_tensor_tensor(out=xT[:, c, :], in0=ps[b][:], scalar=0.5, in1=xT[:, c, :], op0=ALU.mult, op1=ALU.add),
                 reads=[("ps", b), (XT, c)], writes=[(XT, c)])
            S.op("act", lambda e, c=c: e.activation(out=sq[:, c, :], in_=xT[:, c, :], func=AF.Square), reads=[(XT, c)], writes=[("sq", c)])
        norm_stats(rstd, "rstd")
        for k in range(8):
            S.op("dve", lambda e, k=k: e.tensor_tensor(out=hT[:, k, :], in0=xT[:, k, :], in1=rstd[:], op=ALU.mult),
                 reads=[(XT, k), "rstd"], writes=[(HT, k)])

        r_ag, r_av = Ring([0, 1, 2, 3]), Ring([4, 5, 6, 7])
        pend = []

        def ffn_bank(ti, blk, sub, j, acc, acc_id):
            b = zmm(ti, blk, sub, h_rhs, hT_reads)
            u, uid, uhid = ub_get()
            S.op("act", lambda e: e.activation(out=u[:, 1:3], in_=tails_u[:, j, :], func=AF.Copy), reads=[("tu", j)], writes=[uhid])
            S.op("act", lambda e: e.activation(out=u[:, 3:515], in_=ps[b][:], func=AF.Copy), reads=[("ps", b)], writes=[uid])
            S.op("act", lambda e: e.activation(out=acc[:], in_=ps[b][:], func=AF.Identity, scale=pcol(R_FCW + 96 + j), bias=pcol(R_FCB + j)),
                 reads=[("ps", b), "prm"], writes=[acc_id])
            S.op("act", lambda e: e.activation(out=tails_u[:, j, :], in_=ps[b][:, 510:512], func=AF.Copy), reads=[("ps", b)], writes=[("tu", j)])
            S.op("dve", lambda e: e.scalar_tensor_tensor(out=acc[:], in0=u[:, 2:514], scalar=pcol(R_FCW + 48 + j), in1=acc[:], op0=ALU.mult, op1=ALU.add),
                 reads=[uid, uhid, acc_id, "prm"], writes=[acc_id])
            S.op("dve", lambda e: e.scalar_tensor_tensor(out=acc[:], in0=u[:, 1:513], scalar=pcol(R_FCW + 0 + j), in1=acc[:], op0=ALU.mult, op1=ALU.add),
                 reads=[uid, uhid, acc_id, "prm"], writes=[acc_id])


        for e_ in range(12):
            ti = TE[e_]
            wuse(ti)
            for q in range(2):
                j = 2 * e_ + q
                accg, accg_id = r_ag.get()
                accv, accv_id = r_av.get()
                ffn_bank(ti, 0, q, j, accg, accg_id)
                ffn_bank(ti, 1, q, 24 + j, accv, accv_id)
                if pend:
                    pend.pop(0)()

                def fin(accg=accg, accg_id=accg_id, accv=accv, accv_id=accv_id, j=j):
                    S.op("act", lambda e: e.activation(out=accg[:], in_=accg[:], func=AF.Gelu_apprx_tanh), reads=[accg_id], writes=[accg_id])
                    S.op("pool", lambda e: e.tensor_tensor(out=bigbf(j), in0=accg[:], in1=accv[:], op=ALU.mult),
                         reads=[accg_id, accv_id], writes=[("big", j)])
                pend.append(fin)
            wdone(ti)
            if s0a_next and e_ < 8:
                s0a_next[e_]()
        while pend:
            pend.pop(0)()
        if s0a_next:
            part_S0a2(n + 1)
            part_S0b(n + 1)

        def ple_e_part():
            for f in range(2):
                b = newbank()

                def trp(e, f=f, b=b):
                    for tt in range(4):
                        ins = e.transpose(out=ps[b][:, tt * 128:(tt + 1) * 128], in_=pst[:, tt, f * 128:(f + 1) * 128], identity=ident[:])
                    return ins
                S.op("pe", trp, reads=["pst", "ident"], writes=[("ps", b)])
                S.op("dve", lambda e, f=f, b=b: e.tensor_copy(out=pT[:, f, :], in_=ps[b][:]), reads=[("ps", b)], writes=[("pT", f)])
            for c in range(8):
                b = newbank()
                mm_group(b, [(wpp[:, k, c * 128:(c + 1) * 128], pT[:, k, :]) for k in range(2)], reads=["wpp", ("pT", 0), ("pT", 1)])
                S.op("act", lambda e, b=b, c=c: e.activation(out=sq[:, c, :], in_=ps[b][:], func=AF.Square), reads=[("ps", b)], writes=[("sq", c)])
                S.op("act", lambda e, b=b, c=c: e.activation(out=xc[:, c, :], in_=ps[b][:], func=AF.Identity, scale=pcol(C_HGP + c)),
                     reads=[("ps", b), "prm"], writes=[("xc", c)])


        for g in range(2):
            if g == 1:
                ple_e_part()
            banks = [newbank() for _ in range(4)]
            for kt in range(4):
                ti = TF[g][kt]
                wuse(ti)
                lts = [[wt(ti, kk, 0, jq)[0] for kk in range(6)] for jq in range(4)]
                wid = wt(ti, 0, 0, 0)[1]

                def dn(e, lts=lts, kt=kt, banks=banks):
                    for jq in range(4):
                        for kk in range(6):
                            ins = e.matmul(ps[banks[jq]][:], lhsT=lts[jq][kk], rhs=bigbf(kt * 6 + kk), start=(kt == 0 and kk == 0), stop=(kt == 3 and kk == 5))
                    return ins
                if kt < 3:
                    S.op("pe", dn, reads=[("big", kt * 6 + kk) for kk in range(6)] + [wid], writes=[("ps", b) for b in banks])
                else:
                    for jq in range(4):
                        def dn1(e, lt=lts[jq], bk=banks[jq]):
                            for kk in range(6):
                                ins = e.matmul(ps[bk][:], lhsT=lt[kk], rhs=bigbf(18 + kk), start=False, stop=(kk == 5))
                            return ins
                        S.op("pe", dn1, reads=[("big", 18 + kk) for kk in range(6)] + [wid], writes=[("ps", banks[jq])])
                wdone(ti)
            for jq in range(4):
                c = 4 * g + jq
                S.op("dve", lambda e, b=banks[jq], c=c: e.tensor_tensor(out=xT[:, c, :], in0=ps[b][:], in1=xT[:, c, :], op=ALU.add),
                     reads=[("ps", banks[jq]), (XT, c)], writes=[(XT, c)])
                S.op("act", lambda e, c=c: e.activation(out=hT[:, c, :], in_=xT[:, c, :], func=AF.Copy), reads=[(XT, c)], writes=[(HT, c)])
        norm_stats(rstd_e, "rstd_e")

    def ybuf(c):
        i0 = 2 * c if c < 4 else 16 + 2 * (c - 4)
        return big[:, i0 * 256:(i0 + 2) * 256], [("big", i0), ("big", i0 + 1)]

    def part_G1(n):
        t0 = n * T
        xT = xTs[n % 2]
        XT = "xT%d" % (n % 2)
        hT, HT, hT_reads, h_rhs = hctx(n)
        r_tg2 = Ring([0, 1, 2])
        pend = []
        for c in range(8):
            if c % 4 == 0:
                wuse(TGt[c // 4])
            b = zmm(TGt[c // 4], 0, c % 4, h_rhs, hT_reads)
            if c % 4 == 3:
                wdone(TGt[c // 4])
            tg, tg_id = r_tg2.get()
            S.op("act", lambda e, tg=tg, b=b: e.activation(out=tg[:], in_=ps[b][:], func=AF.Tanh, scale=0.5), reads=[("ps", b)], writes=[tg_id])
            if pend:
                pend.pop(0)()
            S.op("dve", lambda e, tg=tg, c=c: e.scalar_tensor_tensor(out=tg[:], in0=tg[:], scalar=1.0, in1=xc[:, c, :], op0=ALU.add, op1=ALU.mult),
                 reads=[tg_id, ("xc", c)], writes=[tg_id])
            S.op("dve", lambda e, tg=tg: e.tensor_tensor(out=tg[:], in0=tg[:], in1=rstd_e[:], op=ALU.mult), reads=[tg_id, "rstd_e"], writes=[tg_id])
            S.op("dve", lambda e, tg=tg, c=c: e.tensor_tensor(out=xT[:, c, :], in0=tg[:], in1=xT[:, c, :], op=ALU.add),
                 reads=[tg_id, (XT, c)], writes=[(XT, c)])
            pend.append(lambda c=c: S.op("pool", lambda e: e.tensor_tensor(out=bigbf(8 + c), in0=xT[:, c, :], in1=xT[:, c, :], op=ALU.mult), reads=[(XT, c)], writes=[("big", 8 + c)]))
        while pend:
            pend.pop(0)()

    def part_G1b(n):
        xT = xTs[n % 2]
        XT = "xT%d" % (n % 2)
        norm_stats(rstd, "rstd", alt=True)
        for c in range(8):
            yb_, yids = ybuf(c)
            S.op("dve", lambda e, c=c, yb_=yb_: e.scalar_tensor_tensor(out=yb_, in0=xT[:, c, :], scalar=pcol(R_GFIN + c), in1=rstd[:], op0=ALU.mult, op1=ALU.mult),
                 reads=[(XT, c), "rstd", "prm"], writes=yids)

    def part_G2(n):
        t0 = n * T
        for tt in range(4):
            ob = xs[tt % 2]
            oid = ("xs", tt % 2)
            for g in range(2):
                b = newbank()

                def tro(e, g=g, b=b, tt=tt):
                    for q in range(4):
                        ins = e.transpose(out=ps[b][:, q * 128:(q + 1) * 128], in_=ybuf(4 * g + q)[0][:, tt * 128:(tt + 1) * 128], identity=ident[:])
                    return ins
                S.op("pe", tro, reads=[i_ for q in range(4) for i_ in ybuf(4 * g + q)[1]] + ["ident"], writes=[("ps", b)])
                if g == 0:
                    S.op("act", lambda e, ob=ob, b=b: e.activation(out=ob[:, 0:512], in_=ps[b][:], func=AF.Copy), reads=[("ps", b)], writes=[oid])
                else:
                    S.op("dve", lambda e, ob=ob, b=b: e.tensor_copy(out=ob[:, 512:1024], in_=ps[b][:]), reads=[("ps", b)], writes=[oid])
            S.dma("sp", lambda e, ob=ob, tt=tt: [e.dma_start(out=out[t0 + tt * 128:t0 + (tt + 1) * 128, :], in_=ob[:])], "xs%d" % (tt % 2), 1,
                  reads=[oid])

    c0_load(0)
    c0_load(1)
    c0_cast(0)
    part_S0a(0)
    part_S0a2(0)
    part_S0b(0)
    for _ in loop1_gen(0):
        pass
    for n in range(NCH):
        main_rest(n)
        more = n + 1 < NCH
        if more and n == 0:
            part_S0a(1)
        part_G1(n)
        if more:
            if n == 0:
                part_S0a2(1)
                part_S0b(1)
            g1 = loop1_gen(n + 1)
            for _ in range(3):
                next(g1)
        part_G1b(n)
        if more:
            for _ in range(3):
                next(g1)
        part_G2(n)
        if more:
            for _ in g1:
                pass

    S.emit(final_waits=["xs0", "xs1"])
    return nc


_NC_CACHE = {}


def kernel(x, p, g_mix, w_in, rnn_conv_w, rnn_conv_b, w_rg_a, b_rg_a, w_rg_x, b_rg_x,
           lru_lambda, sc_conv_w, w_proj_a, w_proj_b, w_out, g_ffn, w_up, ffn_conv_w,
           ffn_conv_b, w_down, w_ple_gate, w_ple_proj, g_ple, g_final):
    f = lambda a: np.ascontiguousarray(np.asarray(a, dtype=np.float32))
    x = f(x)
    p = f(p)
    B, S_TOK, _ = x.shape
    if S_TOK not in _NC_CACHE:
        _NC_CACHE[S_TOK] = build(S_TOK)
    nc = _NC_CACHE[S_TOK]
    shared = {
        "g_mix": f(g_mix)[0], "w_in": f(w_in)[0], "rnn_conv_w": f(rnn_conv_w)[0], "rnn_conv_b": f(rnn_conv_b)[0],
        "w_rg_a": f(w_rg_a)[0], "b_rg_a": f(b_rg_a)[0], "w_rg_x": f(w_rg_x)[0], "b_rg_x": f(b_rg_x)[0],
        "lru_lambda": f(lru_lambda)[0], "sc_conv_w": f(sc_conv_w)[0], "w_proj_a": f(w_proj_a)[0],
        "w_proj_b": f(w_proj_b)[0], "w_out": f(w_out)[0], "g_ffn": f(g_ffn)[0], "w_up": f(w_up)[0],
        "ffn_conv_w": f(ffn_conv_w)[0], "ffn_conv_b": f(ffn_conv_b)[0], "w_down": f(w_down)[0],
        "w_ple_gate": f(w_ple_gate)[0], "w_ple_proj": f(w_ple_proj)[0], "g_ple": f(g_ple)[0],
        "g_final": f(g_final), "ident": np.eye(128, dtype=np.float32),
    }
    in_maps = []
    for c in range(B):
        m = dict(shared)
        m["x"] = x[c]
        m["p"] = p[0, c]
        in_maps.append(m)
    res = run_bass_kernel_spmd(nc, in_maps, core_ids=list(range(B)))
    return np.stack([np.asarray(r["out"], dtype=np.float32) for r in res.results], axis=0)
```

```python
import contextlib
import numpy as np
import concourse.bass as bass
import concourse.mybir as mybir
from concourse.bass_utils import run_bass_kernel_spmd

F32 = mybir.dt.float32
BF16 = mybir.dt.bfloat16
AF = mybir.ActivationFunctionType
ALU = mybir.AluOpType

ENG_NAMES = ("pe", "act", "dve", "pool", "sp")
N_CORES = 8
D = 1024
T = 512
NS = 5
EPS = 1e-6


class Sched:
    def __init__(self, nc):
        self.nc = nc
        self.ops = {e: [] for e in ENG_NAMES}
        self.last_w = {}
        self.readers = {}
        self.dma_cnt = {}

    def _deps_for(self, eng, reads, writes):
        deps = []
        for t in reads:
            w = self.last_w.get(t)
            if w is not None:
                deps.append(w)
            if isinstance(t, tuple) and t[0] == "ps":
                deps.extend(r for r in self.readers.get(t, ()) if r[1] != eng)
        for t in writes:
            w = self.last_w.get(t)
            if w is not None:
                deps.append(w)
            deps.extend(self.readers.get(t, ()))
        return deps

    def _commit(self, tok, reads, writes):
        for t in reads:
            self.readers.setdefault(t, []).append(tok)
        for t in writes:
            self.last_w[t] = tok
            self.readers[t] = []

    def op(self, eng, fn, reads=(), writes=()):
        reads = list(reads)
        writes = list(writes)
        deps = self._deps_for(eng, reads, writes)
        idx = len(self.ops[eng])
        tok = ("eng", eng, idx)
        self.ops[eng].append(dict(fn=fn, deps=deps, dma=None, signal=False))
        self._commit(tok, reads, writes)
        return tok

    def dma(self, eng, fn, key, n, reads=(), writes=()):
        reads = list(reads)
        writes = list(writes)
        deps = self._deps_for(eng, reads, writes)
        self.dma_cnt[key] = self.dma_cnt.get(key, 0) + 16 * n
        tok = ("dma", key, self.dma_cnt[key])
        self.ops[eng].append(dict(fn=fn, deps=deps, dma=key, signal=False))
        self._commit(tok, reads, writes)
        return tok

    def emit(self, final_waits=()):
        nc = self.nc
        for e in ENG_NAMES:
            for o in self.ops[e]:
                for d in o["deps"]:
                    if d[0] == "eng":
                        self.ops[d[1]][d[2]]["signal"] = True
        cnt_at = {}
        for e in ENG_NAMES:
            c = 0
            for i, o in enumerate(self.ops[e]):
                if o["signal"]:
                    c += 1
                cnt_at[(e, i)] = c
        with contextlib.ExitStack() as st:
            esem = {e: st.enter_context(nc.semaphore("s_" + e)) for e in ENG_NAMES}
            dsem = {k: st.enter_context(nc.semaphore("d_" + str(k))) for k in self.dma_cnt}
            block = st.enter_context(nc.Block())

            def body(e, eng):
                waited = {}
                for i, o in enumerate(self.ops[e]):
                    need = {}
                    for d in o["deps"]:
                        if d[0] == "eng":
                            s, v = ("e", d[1]), cnt_at[(d[1], d[2])]
                        else:
                            s, v = ("d", d[1]), d[2]
                        if v > need.get(s, 0):
                            need[s] = v
                    for s, v in need.items():
                        if v > waited.get(s, 0):
                            waited[s] = v
                            sem = esem[s[1]] if s[0] == "e" else dsem[s[1]]
                            eng.wait_ge(sem, v)
                    r = o["fn"](eng)
                    if o["dma"] is not None:
                        for ins in r:
                            ins.then_inc(dsem[o["dma"]], 16)
                    elif o["signal"]:
                        r.then_inc(esem[e], 1)
                if e == "sp":
                    for k in final_waits:
                        eng.wait_ge(dsem[k], self.dma_cnt[k])

            @block.tensor
            def _(eng):
                body("pe", eng)

            @block.scalar
            def _(eng):
                body("act", eng)

            @block.vector
            def _(eng):
                body("dve", eng)

            @block.gpsimd
            def _(eng):
                body("pool", eng)

            @block.sync
            def _(eng):
                body("sp", eng)


R_GMIX, R_RCW, R_RCB, R_BRA, R_BRX, R_LAM, R_SCW, R_GFFN, R_FCW, R_FCB, R_GPLE, R_GFIN = (
    0, 8, 40, 48, 56, 64, 72, 96, 104, 248, 296, 304)
N_ROWS = 312
C_HBA, C_HBX, C_CL, C_HCL, C_HGP, C_EPS, C_QUART, C_E, C_T1 = 312, 320, 328, 336, 344, 352, 353, 354, 362
N_PRM = 372
NPOOL = 12
NUB = 2


def build(S_TOK=4096):
    NCH = S_TOK // T
    nc = bass.Bass("TRN2", target_bir_lowering=False)

    def din(name, shape):
        return nc.dram_tensor(name, list(shape), F32, kind="ExternalInput").ap()

    x = din("x", [S_TOK, D])
    p = din("p", [S_TOK, 256])
    g_mix = din("g_mix", [1024])
    w_in = din("w_in", [1024, 7168])
    rnn_conv_w = din("rnn_conv_w", [4, 1024])
    rnn_conv_b = din("rnn_conv_b", [1024])
    w_rg_a = din("w_rg_a", [8, 128, 128])
    b_rg_a = din("b_rg_a", [1024])
    w_rg_x = din("w_rg_x", [8, 128, 128])
    b_rg_x = din("b_rg_x", [1024])
    lru_lambda = din("lru_lambda", [1024])
    sc_conv_w = din("sc_conv_w", [3, 1024])
    w_proj_a = din("w_proj_a", [1024, 1024])
    w_proj_b = din("w_proj_b", [1024, 1024])
    w_out = din("w_out", [1024, 1024])
    g_ffn = din("g_ffn", [1024])
    w_up = din("w_up", [1024, 6144])
    ffn_conv_w = din("ffn_conv_w", [3, 6144])
    ffn_conv_b = din("ffn_conv_b", [6144])
    w_down = din("w_down", [3072, 1024])
    w_ple_gate = din("w_ple_gate", [1024, 1024])
    w_ple_proj = din("w_ple_proj", [256, 1024])
    g_ple = din("g_ple", [1024])
    g_final = din("g_final", [1024])
    ident_d = din("ident", [128, 128])
    out = nc.dram_tensor("out", [S_TOK, D], F32, kind="ExternalOutput").ap()

    tiles = []

    def mk(blocks, k0=0, nk=8, srow=None, nsc=None):
        tc_ = sum(b[2] for b in blocks)
        tiles.append(dict(blocks=blocks, k0=k0, nk=nk, srow=srow, tc=tc_, nsc=(tc_ if nsc is None else nsc)))
        return len(tiles) - 1

    TA = [mk([(w_in, h * 512, 512)], srow=R_GMIX) for h in range(2)]
    TB = [mk([(w_in, 4096 + c * 128, 128), (w_in, 3072 + c * 128, 128), (w_in, 2048 + c * 128, 128)], srow=R_GMIX) for c in range(8)]
    TG = [mk([(w_in, 1024 + h * 512, 512)], srow=R_GMIX) for h in range(2)]
    TC = [mk([(w_in, 5120 + c * 128, 128), (w_in, 6144 + c * 128, 128), (w_proj_a, c * 128, 128), (w_proj_b, c * 128, 128)],
             srow=R_GMIX, nsc=256) for c in range(8)]
    TD = [mk([(w_out, h * 512, 512)]) for h in range(2)]
    TE = [mk([(w_up, (2 * e) * 128, 256), (w_up, 3072 + (2 * e) * 128, 256)], srow=R_GFFN) for e in range(12)]
    TF = [[mk([(w_down, g * 512, 512)], k0=kt * 6, nk=6) for kt in range(4)] for g in range(2)]
    TGt = [mk([(w_ple_gate, h * 512, 512)]) for h in range(2)]
    N_WT = len(tiles)
    wscr = nc.dram_tensor("wscr", [N_WT, 128, 4096], BF16, kind="Internal").ap()

    sb = nc.alloc_sbuf_tensor
    wslot = [sb("wslot%d" % s, [128, 4096], BF16) for s in range(NS)]
    stage = [sb("stage%d" % s, [128, 2048], F32) for s in range(2)]
    rgw = sb("rgw", [128, 16, 128], BF16)
    wpp = sb("wpp", [128, 2, 1024], BF16)
    ident = sb("ident_sb", [128, 128], F32)
    ones = sb("ones_sb", [128, 128], BF16)
    prm_rows = sb("prm_rows", [128, 3, 128], F32)
    prm = sb("prm", [128, N_PRM], F32)
    xs = [sb("xs%d" % i, [128, 1024], F32) for i in range(2)]
    pst = sb("pst", [128, 4, 256], F32)
    pT = sb("pT", [128, 2, 512], BF16)
    xTs = [sb("xT%d" % i, [128, 8, 512], F32) for i in range(2)]
    hT = sb("hT", [128, 8, 512], BF16)
    sq = sb("sq", [128, 8, 512], BF16)
    xcb = sq
    xc = sb("xc", [128, 8, 512], F32)
    big = sb("big", [128, 6144], F32)
    bigb = big[:].bitcast(BF16)
    tails_xr = sb("tails_xr", [128, 8, 3], F32)
    tails_q = sb("tails_q", [128, 8, 2], F32)
    tails_u = sb("tails_u", [128, 48, 2], F32)
    hstate = sb("hstate", [128, 8], F32)
    ub = [sb("ub%d" % i, [128, 516], F32) for i in range(NUB)]
    rt = sb("rt", [128, 512], F32)
    rstd = sb("rstd", [128, 512], F32)
    rstd_e = sb("rstd_e", [128, 512], F32)
    rstd1 = sb("rstd1", [128, 512], F32)
    tpool = [sb("tp%d" % i, [128, 512], F32) for i in range(NPOOL)]
    ps = [nc.alloc_psum_tensor("ps%d" % b, [128, 512], F32) for b in range(8)]

    S = Sched(nc)

    class Ring:
        def __init__(self, idxs):
            self.idxs = list(idxs)
            self.i = 0

        def get(self):
            k = self.idxs[self.i % len(self.idxs)]
            self.i += 1
            return tpool[k], ("P", k)

    bank_ctr = [0]

    def newbank():
        b = bank_ctr[0] % 8
        bank_ctr[0] += 1
        return b

    def bigbf(j):
        return bigb[:, j * 512:(j + 1) * 512]

    def mm_group(bank, pairs, reads):
        def fn(e):
            n = len(pairs)
            for i, (l, r) in enumerate(pairs):
                ins = e.matmul(ps[bank][:], lhsT=l, rhs=r, start=(i == 0), stop=(i == n - 1))
            return ins
        S.op("pe", fn, reads=reads, writes=[("ps", bank)])

    def pcol(c):
        return prm[:, c:c + 1]

    S.dma("sp", lambda e: [e.dma_start(out=ident[:], in_=ident_d)], "c_id", 1, writes=["ident"])
    S.op("pool", lambda e: e.memset(ones[:], 1.0), writes=["ones"])
    S.op("pool", lambda e: e.memset(tails_xr[:], 0.0), writes=[("txr", c) for c in range(8)])
    S.op("pool", lambda e: e.memset(tails_q[:], 0.0), writes=[("tq", c) for c in range(8)])
    S.op("pool", lambda e: e.memset(tails_u[:], 0.0), writes=[("tu", c) for c in range(48)])
    S.op("pool", lambda e: e.memset(hstate[:], 0.0), writes=[("hst", c) for c in range(8)])
    S.op("pool", lambda e: e.memset(prm_rows[:], 0.0), writes=["prm_rows"])
    S.op("pool", lambda e: e.memset(prm[:], 0.0), writes=["prm"])

    plist = [(g_mix, R_GMIX, 8), (rnn_conv_w, R_RCW, 32), (rnn_conv_b, R_RCB, 8), (b_rg_a, R_BRA, 8),
             (b_rg_x, R_BRX, 8), (lru_lambda, R_LAM, 8), (sc_conv_w, R_SCW, 24), (g_ffn, R_GFFN, 8),
             (ffn_conv_w, R_FCW, 144), (ffn_conv_b, R_FCB, 48), (g_ple, R_GPLE, 8), (g_final, R_GFIN, 8)]
    pieces = []
    for ap, r0, n in plist:
        flat = ap if len(ap.shape) == 1 else ap.rearrange("a b -> (a b)")
        rows = flat.rearrange("(r q) -> r q", q=128)
        done = 0
        while done < n:
            r = r0 + done
            take = min(n - done, 128 - (r % 128))
            pieces.append((prm_rows[(r % 128):(r % 128) + take, r // 128, :], rows[done:done + take, :]))
            done += take
    S.dma("sp", lambda e: [e.dma_start(out=o, in_=i) for o, i in pieces], "c_prm", len(pieces), writes=["prm_rows"])

    def prm_tr(e):
        for s_, n in ((0, 128), (1, 128), (2, N_ROWS - 256)):
            ins = e.transpose(out=ps[0][:, s_ * 128:s_ * 128 + n], in_=prm_rows[0:n, s_, :], identity=ident[0:n, 0:n])
        return ins
    bank_ctr[0] = 1
    S.op("pe", prm_tr, reads=["prm_rows", "ident"], writes=[("ps", 0)])
    S.op("dve", lambda e: e.tensor_copy(out=prm[:, 0:N_ROWS], in_=ps[0][:, 0:N_ROWS]), reads=[("ps", 0)], writes=["prm"])

    def sl(c, n=8):
        return prm[:, c:c + n]

    def pop(eng, fn):
        S.op(eng, fn, reads=["prm"], writes=["prm"])
    pop("dve", lambda e: e.tensor_scalar(out=sl(C_HBA), in0=sl(R_BRA), scalar1=0.5, scalar2=None, op0=ALU.mult))
    pop("dve", lambda e: e.tensor_scalar(out=sl(C_HBX), in0=sl(R_BRX), scalar1=0.5, scalar2=None, op0=ALU.mult))
    pop("dve", lambda e: e.tensor_scalar(out=sl(C_HGP), in0=sl(R_GPLE), scalar1=0.5, scalar2=None, op0=ALU.mult))
    pop("dve", lambda e: e.memset(sl(C_EPS, 1), EPS))
    pop("dve", lambda e: e.memset(sl(C_QUART, 1), 0.25))
    pop("act", lambda e: e.activation(out=sl(C_E), in_=sl(R_LAM), func=AF.Exp, scale=-1.0))
    pop("dve", lambda e: e.tensor_scalar(out=sl(C_T1), in0=sl(C_E), scalar1=-0.25, scalar2=1.0 / 3.0, op0=ALU.mult, op1=ALU.add))
    pop("dve", lambda e: e.tensor_tensor(out=sl(C_T1), in0=sl(C_T1), in1=sl(C_E), op=ALU.mult))
    pop("dve", lambda e: e.tensor_scalar(out=sl(C_T1), in0=sl(C_T1), scalar1=-1.0, scalar2=0.5, op0=ALU.mult, op1=ALU.add))
    pop("dve", lambda e: e.tensor_tensor(out=sl(C_T1), in0=sl(C_T1), in1=sl(C_E), op=ALU.mult))
    pop("dve", lambda e: e.tensor_scalar(out=sl(C_T1), in0=sl(C_T1), scalar1=-1.0, scalar2=1.0, op0=ALU.mult, op1=ALU.add))
    pop("dve", lambda e: e.tensor_tensor(out=sl(C_T1), in0=sl(C_T1), in1=sl(C_E), op=ALU.mult))
    pop("dve", lambda e: e.tensor_scalar(out=sl(C_CL), in0=sl(C_T1), scalar1=-8.0, scalar2=None, op0=ALU.mult))
    pop("dve", lambda e: e.tensor_scalar(out=sl(C_HCL), in0=sl(C_T1), scalar1=-4.0, scalar2=None, op0=ALU.mult))

    st0, st1 = stage[0], stage[1]
    S.dma("sp", lambda e: [e.dma_start(out=st0[:, 0:1024].rearrange("q (h j) -> q h j", h=8), in_=w_rg_a.rearrange("h i j -> i h j")),
                           e.dma_start(out=st0[:, 1024:2048].rearrange("q (h j) -> q h j", h=8), in_=w_rg_x.rearrange("h i j -> i h j"))],
          "stg0", 2, writes=[("stage", 0)])
    S.op("dve", lambda e: e.tensor_copy(out=rgw[:].rearrange("q h j -> q (h j)"), in_=st0[:, 0:2048]), reads=[("stage", 0)], writes=["rgw"])
    S.dma("sp", lambda e: [e.dma_start(out=st1[:, 0:2048].rearrange("q (k n) -> q k n", k=2), in_=w_ple_proj.rearrange("(k q) n -> q k n", q=128))],
          "stg1", 1, writes=[("stage", 1)])
    S.op("dve", lambda e: e.tensor_copy(out=wpp[:].rearrange("q k n -> q (k n)"), in_=st1[:, 0:2048]), reads=[("stage", 1)], writes=["wpp"])

    cast_rr = [0]
    xT1flat = xTs[1][:].rearrange("q c t -> q (c t)")
    stage_bufs = [(stage[0][:, :], [("stage", 0)]), (stage[1][:, :], [("stage", 1)]),
                  (xT1flat[:, 0:2048], [("xT1", k) for k in range(4)]), (xT1flat[:, 2048:4096], [("xT1", k) for k in range(4, 8)])]

    def cast_op(en, dst_ap, src_ap, scale_col, reads, writes):
        if en == "act":
            if scale_col is None:
                S.op("act", lambda e: e.activation(out=dst_ap, in_=src_ap, func=AF.Copy), reads=reads, writes=writes)
            else:
                S.op("act", lambda e: e.activation(out=dst_ap, in_=src_ap, func=AF.Identity, scale=pcol(scale_col)), reads=reads + ["prm"], writes=writes)
        else:
            if scale_col is None:
                S.op(en, lambda e: e.tensor_copy(out=dst_ap, in_=src_ap), reads=reads, writes=writes)
            else:
                S.op(en, lambda e: e.tensor_scalar(out=dst_ap, in0=src_ap, scalar1=pcol(scale_col), scalar2=None, op0=ALU.mult), reads=reads + ["prm"], writes=writes)

    stream = TA + TB + TG + TC + TD + TE + TF[0] + TF[1] + TGt
    L = len(stream)
    assert sorted(stream) == list(range(N_WT))
    NSTR = L * NCH
    st_next = [L]
    st_used = [0]
    retired = [False] * NSTR
    cur = {}
    PF = 3

    def stage_of(i, half):
        return half if i == L - 1 else (2 * i + half) % 4

    def c0_load(i):
        td = tiles[stream[i]]
        hk, tc_ = td["nk"] // 2, td["tc"]
        for half in range(2):
            g_ = stage_of(i, half)
            stg, sids = stage_bufs[g_]
            stg3 = stg[:, 0:hk * tc_].rearrange("q (k n) -> q k n", k=hk)
            srcs = []
            off = 0
            for (W, c0, ncol) in td["blocks"]:
                kk0 = td["k0"] + half * hk
                srcs.append((stg3[:, :, off:off + ncol], W.rearrange("(k q) n -> q k n", q=128)[:, kk0:kk0 + hk, c0:c0 + ncol]))
                off += ncol
            S.dma("sp", lambda e, srcs=srcs: [e.dma_start(out=o, in_=i_) for o, i_ in srcs], "stg%d" % g_, len(srcs), writes=sids)

    def c0_cast(i):
        ti = stream[i]
        td = tiles[ti]
        hk, tc_ = td["nk"] // 2, td["tc"]
        s_ = i % NS
        assert i < NS or retired[i - NS], "slot not free for chunk-0 cast of stream idx %d" % i
        cur[ti] = (i, s_)
        slot = wslot[s_]
        wid = ("wslot", s_)
        for half in range(2):
            g_ = stage_of(i, half)
            stg, sids = stage_bufs[g_]
            stg3 = stg[:, 0:hk * tc_].rearrange("q (k n) -> q k n", k=hk)
            dst3 = slot[:, half * hk * tc_:(half + 1) * hk * tc_].rearrange("q (k n) -> q k n", k=hk)
            if td["srow"] is None:
                en = ("act", "dve")[cast_rr[0] % 2]
                cast_rr[0] += 1
                cast_op(en, dst3, stg3, None, list(sids), [wid])
            else:
                en = ("act", "dve")[cast_rr[0] % 2]
                cast_rr[0] += 1
                nsc = td["nsc"]
                for kk in range(hk):
                    cast_op(en, dst3[:, kk, 0:nsc], stg3[:, kk, 0:nsc], td["srow"] + td["k0"] + half * hk + kk, list(sids), [wid])
                if nsc < tc_:
                    cast_op("pool", dst3[:, :, nsc:tc_], stg3[:, :, nsc:tc_], None, list(sids), [wid])

    def c0_store(i):
        if NCH == 1:
            return
        ti = stream[i]
        td = tiles[ti]
        n_ = td["nk"] * td["tc"]
        s_ = i % NS
        S.dma("sp", lambda e: [e.dma_start(out=wscr[ti, :, 0:n_], in_=wslot[s_][:, 0:n_])], "cst%d" % s_, 1,
              reads=[("wslot", s_)], writes=[("wscr", ti)])

    def wload_scratch(i):
        ti = stream[i % L]
        s_ = i % NS
        n_ = tiles[ti]["nk"] * tiles[ti]["tc"]
        S.dma("sp", lambda e: [e.dma_start(out=wslot[s_][:, 0:n_], in_=wscr[ti, :, 0:n_])], "wl%d" % s_, 1,
              reads=[("wscr", ti)], writes=[("wslot", s_)])
        cur[ti] = (i, s_)

    def pump():
        while st_next[0] < NSTR and st_next[0] <= st_used[0] + PF and retired[st_next[0] - NS]:
            wload_scratch(st_next[0])
            st_next[0] += 1

    def wuse(ti):
        i = st_used[0]
        assert stream[i % L] == ti, (i, ti)
        if i < L:
            c0_store(i)
            if i + 1 < L:
                c0_cast(i + 1)
            if i + 2 < L:
                c0_load(i + 2)
        else:
            pump()
            assert st_next[0] > i, "weight ring too small / prefetch stalled at stream idx %d" % i
        st_used[0] += 1
        pump()

    def wdone(ti):
        i, _ = cur[ti]
        retired[i] = True
        pump()

    def wt(ti, k, blk, sub=0):
        td = tiles[ti]
        _, s_ = cur[ti]
        off = sum(b[2] for b in td["blocks"][:blk]) + sub * 128
        base = k * td["tc"] + off
        return wslot[s_][:, base:base + 128], ("wslot", s_)

    def zmm(ti, blk, sub, rhs_fn, rhs_reads, nk=8):
        b = newbank()
        pairs, wid = [], None
        for k in range(nk):
            l, wid = wt(ti, k, blk, sub)
            pairs.append((l, rhs_fn(k)))
        mm_group(b, pairs, reads=list(rhs_reads) + [wid])
        return b

    hTs = [hT, stage[0][:].bitcast(BF16).rearrange("q (c t) -> q c t", c=8)]

    def hctx(n):
        hb = hTs[n % 2]
        HT = "hT%d" % (n % 2)
        return hb, HT, [(HT, k) for k in range(8)], (lambda k: hb[:, k, :])

    ub_i = [0]

    def ub_get():
        k_ = ub_i[0] % NUB
        ub_i[0] += 1
        return ub[k_], ("ub", k_), ("ubh", k_)

    def norm_stats(dst_rstd, dst_id, alt=False):
        b = newbank()
        if alt:
            mm_group(b, [(ones[:], bigbf(8 + k)) for k in range(8)], reads=["ones"] + [("big", 8 + k) for k in range(8)])
        else:
            mm_group(b, [(ones[:], sq[:, k, :]) for k in range(8)], reads=["ones"] + [("sq", k) for k in range(8)])
        S.op("act", lambda e: e.activation(out=rt[:], in_=ps[b][:], func=AF.Ln, scale=1.0 / 1024.0, bias=pcol(C_EPS)),
             reads=[("ps", b), "prm"], writes=["rt"])
        S.op("act", lambda e: e.activation(out=dst_rstd[:], in_=rt[:], func=AF.Exp, scale=-0.5), reads=["rt"], writes=[dst_id])

    def S0a_groups(n):
        return [lambda tt=tt, g=g: S0a_group(n, tt, g) for tt in range(4) for g in range(2)]

    def part_S0a(n):
        for f_ in S0a_groups(n):
            f_()

    def S0a_group(n, tt, g):
        t0 = n * T
        xT = xTs[n % 2]
        XT = "xT%d" % (n % 2)
        if True:
            xb = xs[tt % 2]
            xid = ("xs", tt % 2)
            if g == 0:
                S.dma("sp", lambda e, xb=xb, tt=tt: [e.dma_start(out=xb[:], in_=x[t0 + tt * 128:t0 + (tt + 1) * 128, :])],
                      "xs%d" % (tt % 2), 1, writes=[xid])
            if True:
                b = newbank()

                def trf(e, xb=xb, g=g, b=b):
                    for q in range(4):
                        c = 4 * g + q
                        ins = e.transpose(out=ps[b][:, q * 128:(q + 1) * 128], in_=xb[:, c * 128:(c + 1) * 128], identity=ident[:])
                    return ins
                S.op("pe", trf, reads=[xid, "ident"], writes=[("ps", b)])
                S.op("act", lambda e, g=g, b=b, tt=tt: e.activation(out=xT[:, 4 * g:4 * g + 4, tt * 128:(tt + 1) * 128],
                                                                  in_=ps[b][:].rearrange("q (c t) -> q c t", c=4), func=AF.Copy),
                     reads=[("ps", b)], writes=[(XT, 4 * g + q) for q in range(4)])
                S.op("act", lambda e, g=g, b=b, tt=tt: e.activation(out=sq[:, 4 * g:4 * g + 4, tt * 128:(tt + 1) * 128],
                                                                  in_=ps[b][:].rearrange("q (c t) -> q c t", c=4), func=AF.Square),
                     reads=[("ps", b)], writes=[("sq", 4 * g + q) for q in range(4)])

    def part_S0a2(n):
        norm_stats(rstd1, "rstd1")

    def part_S0b(n):
        xT = xTs[n % 2]
        XT = "xT%d" % (n % 2)
        hT, HT, hT_reads, h_rhs = hctx(n)
        extra = [("stage", 0)] if n == 1 else []
        for k in range(8):
            S.op("dve", lambda e, k=k: e.tensor_tensor(out=hT[:, k, :], in0=xT[:, k, :], in1=rstd1[:], op=ALU.mult),
                 reads=[(XT, k), "rstd1"], writes=[(HT, k)] + extra)

    def loop1_gen(n):
        t0 = n * T
        hT, HT, hT_reads, h_rhs = hctx(n)
        S.dma("sp", lambda e: [e.dma_start(out=pst[:], in_=p[t0:t0 + T, :].rearrange("(tt q) f -> q tt f", q=128))], "pst", 1, writes=["pst"])

        r_tr, r_a, r_hm, r_ti = Ring([0]), Ring([1, 2, 3]), Ring([4, 5, 6]), Ring([7, 8, 9])
        r_cx, r_acc = Ring([10]), Ring([11])
        b1 = {}

        def stage_A(c):
            if c % 4 == 0:
                wuse(TA[c // 4])
            b = zmm(TA[c // 4], 0, c % 4, h_rhs, hT_reads)
            if c % 4 == 3:
                wdone(TA[c // 4])
            u, uid, uhid = ub_get()
            S.op("pool", lambda e: e.tensor_copy(out=u[:, 0:3], in_=tails_xr[:, c, :]), reads=[("txr", c)], writes=[uhid])
            S.op("act", lambda e: e.activation(out=u[:, 3:515], in_=ps[b][:], func=AF.Copy), reads=[("ps", b)], writes=[uid])
            S.op("act", lambda e: e.activation(out=xc[:, c, :], in_=ps[b][:], func=AF.Identity, scale=pcol(R_RCW + 24 + c), bias=pcol(R_RCB + c)),
                 reads=[("ps", b), "prm"], writes=[("xc", c)])
            S.op("pool", lambda e: e.tensor_copy(out=tails_xr[:, c, :], in_=u[:, 512:515]), reads=[uid], writes=[("txr", c)])
            for s_ in (1, 2, 3):
                S.op("dve", lambda e, s_=s_: e.scalar_tensor_tensor(out=xc[:, c, :], in0=u[:, 3 - s_:515 - s_], scalar=pcol(R_RCW + (3 - s_) * 8 + c),
                                                                  in1=xc[:, c, :], op0=ALU.mult, op1=ALU.add),
                     reads=[uid, uhid, ("xc", c), "prm"], writes=[("xc", c)])
            S.op("pool", lambda e: e.tensor_copy(out=xcb[:, c, :], in_=xc[:, c, :]), reads=[("xc", c)], writes=[("sq", c)])

        def stage_B1a(c):
            b_ra = newbank()
            mm_group(b_ra, [(rgw[:, c, :], xcb[:, c, :])], reads=["rgw", ("sq", c)])
            b_rx = newbank()
            mm_group(b_rx, [(rgw[:, 8 + c, :], xcb[:, c, :])], reads=["rgw", ("sq", c)])
            tr, tr_id = r_tr.get()
            a_, a_id = r_a.get()
            hm, hm_id = r_hm.get()
            ti_, ti_id = r_ti.get()
            b1[c] = (a_, a_id, hm, hm_id, ti_, ti_id)
            S.op("act", lambda e: e.activation(out=tr[:], in_=ps[b_ra][:], func=AF.Tanh, scale=0.5, bias=pcol(C_HBA + c)),
                 reads=[("ps", b_ra), "prm"], writes=[tr_id])
            S.op("act", lambda e: e.activation(out=ti_[:], in_=ps[b_rx][:], func=AF.Tanh, scale=0.5, bias=pcol(C_HBX + c)),
                 reads=[("ps", b_rx), "prm"], writes=[ti_id])
            S.op("act", lambda e: e.activation(out=a_[:], in_=tr[:], func=AF.Exp, scale=pcol(C_HCL + c), bias=pcol(C_HCL + c)),
                 reads=[tr_id, "prm"], writes=[a_id])
            S.op("act", lambda e: e.activation(out=hm[:], in_=tr[:], func=AF.Exp, scale=pcol(C_CL + c), bias=pcol(C_CL + c)),
                 reads=[tr_id, "prm"], writes=[hm_id])
            S.op("act", lambda e: e.activation(out=hm[:], in_=hm[:], func=AF.Sqrt, scale=-0.25, bias=pcol(C_QUART)),
                 reads=[hm_id, "prm"], writes=[hm_id])

        def stage_B1b(c):
            a_, a_id, hm, hm_id, ti_, ti_id = b1[c]
            S.op("dve", lambda e: e.scalar_tensor_tensor(out=ti_[:], in0=ti_[:], scalar=1.0, in1=xc[:, c, :], op0=ALU.add, op1=ALU.mult),
                 reads=[ti_id, ("xc", c)], writes=[ti_id])
            S.op("pool", lambda e: e.tensor_tensor(out=ti_[:], in0=ti_[:], in1=hm[:], op=ALU.mult), reads=[ti_id, hm_id], writes=[ti_id])

        def stage_B1c(c):
            a_, a_id, hm, hm_id, ti_, ti_id = b1[c]
            S.op("dve", lambda e: e.tensor_tensor_scan(out=xc[:, c, :], data0=a_[:], data1=ti_[:], initial=hstate[:, c:c + 1], op0=ALU.mult, op1=ALU.add),
                 reads=[a_id, ti_id, ("hst", c)], writes=[("xc", c)])
            S.op("pool", lambda e: e.tensor_copy(out=hstate[:, c:c + 1], in_=xc[:, c, 511:512]), reads=[("xc", c)], writes=[("hst", c)])

        def stage_B2(c):
            ti = TB[c]
            wuse(ti)
            b_cx = zmm(ti, 0, 0, h_rhs, hT_reads)
            cxs, cx_id = r_cx.get()
            S.op("act", lambda e: e.activation(out=cxs[:], in_=ps[b_cx][:], func=AF.Copy), reads=[("ps", b_cx)], writes=[cx_id])
            b_cc = zmm(ti, 1, 0, h_rhs, hT_reads)
            u, uid, uhid = ub_get()
            S.op("pool", lambda e: e.tensor_copy(out=u[:, 1:3], in_=tails_q[:, c, :]), reads=[("tq", c)], writes=[uhid])
            S.op("dve", lambda e: e.tensor_tensor(out=u[:, 3:515], in0=ps[b_cc][:], in1=cxs[:], op=ALU.mult),
                 reads=[("ps", b_cc), cx_id], writes=[uid])
            S.op("pool", lambda e: e.tensor_copy(out=tails_q[:, c, :], in_=u[:, 513:515]), reads=[uid], writes=[("tq", c)])
            acc, acc_id = r_acc.get()
            S.op("dve", lambda e: e.tensor_scalar(out=acc[:], in0=u[:, 3:515], scalar1=pcol(R_SCW + 16 + c), scalar2=None, op0=ALU.mult),
                 reads=[uid, "prm"], writes=[acc_id])
            S.op("dve", lambda e: e.scalar_tensor_tensor(out=acc[:], in0=u[:, 2:514], scalar=pcol(R_SCW + 8 + c), in1=acc[:], op0=ALU.mult, op1=ALU.add),
                 reads=[uid, uhid, acc_id, "prm"], writes=[acc_id])
            S.op("dve", lambda e: e.scalar_tensor_tensor(out=acc[:], in0=u[:, 1:513], scalar=pcol(R_SCW + 0 + c), in1=acc[:], op0=ALU.mult, op1=ALU.add),
                 reads=[uid, uhid, acc_id, "prm"], writes=[acc_id])
            b_cb = zmm(ti, 2, 0, h_rhs, hT_reads)
            wdone(ti)
            S.op("dve", lambda e: e.tensor_tensor(out=bigbf(8 + c), in0=ps[b_cb][:], in1=acc[:], op=ALU.mult),
                 reads=[("ps", b_cb), acc_id], writes=[("big", 8 + c)])

        for t_ in range(12):
            if t_ < 8:
                stage_A(t_)
            if t_ >= 4:
                stage_B2(t_ - 4)
            if 2 <= t_ < 10:
                stage_B1a(t_ - 2)
            if 3 <= t_ < 11:
                stage_B1b(t_ - 3)
            if 4 <= t_ < 12:
                stage_B1c(t_ - 4)
            yield t_

    def main_rest(n):
        xT = xTs[n % 2]
        XT = "xT%d" % (n % 2)
        s0a_next = S0a_groups(n + 1) if (n + 1 < NCH and n > 0) else None
        hT, HT, hT_reads, h_rhs = hctx(n)
        r_gg = Ring([0, 1, 2])
        for c in range(8):
            if c % 4 == 0:
                wuse(TG[c // 4])
            b_gr = zmm(TG[c // 4], 0, c % 4, h_rhs, hT_reads)
            if c % 4 == 3:
                wdone(TG[c // 4])
            gg, gg_id = r_gg.get()
            S.op("act", lambda e, gg=gg, b=b_gr: e.activation(out=gg[:], in_=ps[b][:], func=AF.Gelu_apprx_tanh), reads=[("ps", b_gr)], writes=[gg_id])
            S.op("dve", lambda e, gg=gg, c=c: e.tensor_tensor(out=bigbf(c), in0=xc[:, c, :], in1=gg[:], op=ALU.mult),
                 reads=[("xc", c), gg_id], writes=[("big", c)])

        ya_reads = [("big", k) for k in range(8)]
        yb_reads = [("big", 8 + k) for k in range(8)]
        r_tg = Ring([3, 4, 5, 6, 7, 8])
        for c in range(8):
            ti = TC[c]
            wuse(ti)
            parts = []
            for (gblk, pblk, yoff, yreads) in ((0, 2, 0, ya_reads), (1, 3, 8, yb_reads)):
                b_g = zmm(ti, gblk, 0, h_rhs, hT_reads)
                tg, tg_id = r_tg.get()
                S.op("act", lambda e, tg=tg, b=b_g: e.activation(out=tg[:], in_=ps[b][:], func=AF.Tanh, scale=0.5), reads=[("ps", b_g)], writes=[tg_id])
                b_p = zmm(ti, pblk, 0, lambda k, yoff=yoff: bigbf(yoff + k), yreads)
                S.op("dve", lambda e, tg=tg, b=b_p: e.scalar_tensor_tensor(out=tg[:], in0=tg[:], scalar=1.0, in1=ps[b][:], op0=ALU.add, op1=ALU.mult),
                     reads=[tg_id, ("ps", b_p)], writes=[tg_id])
                parts.append((tg, tg_id))
            wdone(ti)
            (ma, ma_id), (mb, mb_id) = parts
            S.op("pool", lambda e, ma=ma, mb=mb, c=c: e.tensor_tensor(out=bigbf(16 + c), in0=ma[:], in1=mb[:], op=ALU.add),
                 reads=[ma_id, mb_id], writes=[("big", 16 + c)])

        m_reads = [("big", 16 + k) for k in range(8)]
        for c in range(8):
            if c % 4 == 0:
                wuse(TD[c // 4])
            b = zmm(TD[c // 4], 0, c % 4, lambda k: bigbf(16 + k), m_reads)
            if c % 4 == 3:
                wdone(TD[c // 4])
            S.op("dve", lambda e, b=b, c=c: e.scalar_tensor_tensor(out=xT[:, c, :], in0=ps[b][:], scalar=0.5, in1=xT[:, c, :], op0=ALU.mult, op1=ALU.add),
                 reads=[("ps", b), (XT, c)], writes=[(XT, c)])
            S.op("act", lambda e, c=c: e.activation(out=sq[:, c, :], in_=xT[:, c, :], func=AF.Square), reads=[(XT, c)], writes=[("sq", c)])
        norm_stats(rstd, "rstd")
        for k in range(8):
            S.op("dve", lambda e, k=k: e.tensor_tensor(out=hT[:, k, :], in0=xT[:, k, :], in1=rstd[:], op=ALU.mult),
                 reads=[(XT, k), "rstd"], writes=[(HT, k)])

        r_ag, r_av = Ring([0, 1, 2, 3]), Ring([4, 5, 6, 7])
        pend = []

        def ffn_bank(ti, blk, sub, j, acc, acc_id):
            b = zmm(ti, blk, sub, h_rhs, hT_reads)
            u, uid, uhid = ub_get()
            S.op("act", lambda e: e.activation(out=u[:, 1:3], in_=tails_u[:, j, :], func=AF.Copy), reads=[("tu", j)], writes=[uhid])
            S.op("act", lambda e: e.activation(out=u[:, 3:515], in_=ps[b][:], func=AF.Copy), reads=[("ps", b)], writes=[uid])
            S.op("act", lambda e: e.activation(out=acc[:], in_=ps[b][:], func=AF.Identity, scale=pcol(R_FCW + 96 + j), bias=pcol(R_FCB + j)),
                 reads=[("ps", b), "prm"], writes=[acc_id])
            S.op("act", lambda e: e.activation(out=tails_u[:, j, :], in_=ps[b][:, 510:512], func=AF.Copy), reads=[("ps", b)], writes=[("tu", j)])
            S.op("dve", lambda e: e.scalar_tensor_tensor(out=acc[:], in0=u[:, 2:514], scalar=pcol(R_FCW + 48 + j), in1=acc[:], op0=ALU.mult, op1=ALU.add),
                 reads=[uid, uhid, acc_id, "prm"], writes=[acc_id])
            S.op("dve", lambda e: e.scalar_tensor_tensor(out=acc[:], in0=u[:, 1:513], scalar=pcol(R_FCW + 0 + j), in1=acc[:], op0=ALU.mult, op1=ALU.add),
                 reads=[uid, uhid, acc_id, "prm"], writes=[acc_id])


        for e_ in range(12):
            ti = TE[e_]
            wuse(ti)
            for q in range(2):
                j = 2 * e_ + q
                accg, accg_id = r_ag.get()
                accv, accv_id = r_av.get()
                ffn_bank(ti, 0, q, j, accg, accg_id)
                ffn_bank(ti, 1, q, 24 + j, accv, accv_id)
                if pend:
                    pend.pop(0)()

                def fin(accg=accg, accg_id=accg_id, accv=accv, accv_id=accv_id, j=j):
                    S.op("act", lambda e: e.activation(out=accg[:], in_=accg[:], func=AF.Gelu_apprx_tanh), reads=[accg_id], writes=[accg_id])
                    S.op("pool", lambda e: e.tensor_tensor(out=bigbf(j), in0=accg[:], in1=accv[:], op=ALU.mult),
                         reads=[accg_id, accv_id], writes=[("big", j)])
                pend.append(fin)
            wdone(ti)
            if s0a_next and e_ < 8:
                s0a_next[e_]()
        while pend:
            pend.pop(0)()
        if s0a_next:
            part_S0a2(n + 1)
            part_S0b(n + 1)

        def ple_e_part():
            for f in range(2):
                b = newbank()

                def trp(e, f=f, b=b):
                    for tt in range(4):
                        ins = e.transpose(out=ps[b][:, tt * 128:(tt + 1) * 128], in_=pst[:, tt, f * 128:(f + 1) * 128], identity=ident[:])
                    return ins
                S.op("pe", trp, reads=["pst", "ident"], writes=[("ps", b)])
                S.op("dve", lambda e, f=f, b=b: e.tensor_copy(out=pT[:, f, :], in_=ps[b][:]), reads=[("ps", b)], writes=[("pT", f)])
            for c in range(8):
                b = newbank()
                mm_group(b, [(wpp[:, k, c * 128:(c + 1) * 128], pT[:, k, :]) for k in range(2)], reads=["wpp", ("pT", 0), ("pT", 1)])
                S.op("act", lambda e, b=b, c=c: e.activation(out=sq[:, c, :], in_=ps[b][:], func=AF.Square), reads=[("ps", b)], writes=[("sq", c)])
                S.op("act", lambda e, b=b, c=c: e.activation(out=xc[:, c, :], in_=ps[b][:], func=AF.Identity, scale=pcol(C_HGP + c)),
                     reads=[("ps", b), "prm"], writes=[("xc", c)])


        for g in range(2):
            if g == 1:
                ple_e_part()
            banks = [newbank() for _ in range(4)]
            for kt in range(4):
                ti = TF[g][kt]
                wuse(ti)
                lts = [[wt(ti, kk, 0, jq)[0] for kk in range(6)] for jq in range(4)]
                wid = wt(ti, 0, 0, 0)[1]

                def dn(e, lts=lts, kt=kt, banks=banks):
                    for jq in range(4):
                        for kk in range(6):
                            ins = e.matmul(ps[banks[jq]][:], lhsT=lts[jq][kk], rhs=bigbf(kt * 6 + kk), start=(kt == 0 and kk == 0), stop=(kt == 3 and kk == 5))
                    return ins
                if kt < 3:
                    S.op("pe", dn, reads=[("big", kt * 6 + kk) for kk in range(6)] + [wid], writes=[("ps", b) for b in banks])
                else:
                    for jq in range(4):
                        def dn1(e, lt=lts[jq], bk=banks[jq]):
                            for kk in range(6):
                                ins = e.matmul(ps[bk][:], lhsT=lt[kk], rhs=bigbf(18 + kk), start=False, stop=(kk == 5))
                            return ins
                        S.op("pe", dn1, reads=[("big", 18 + kk) for kk in range(6)] + [wid], writes=[("ps", banks[jq])])
                wdone(ti)
            for jq in range(4):
                c = 4 * g + jq
                S.op("dve", lambda e, b=banks[jq], c=c: e.tensor_tensor(out=xT[:, c, :], in0=ps[b][:], in1=xT[:, c, :], op=ALU.add),
                     reads=[("ps", banks[jq]), (XT, c)], writes=[(XT, c)])
                S.op("act", lambda e, c=c: e.activation(out=hT[:, c, :], in_=xT[:, c, :], func=AF.Copy), reads=[(XT, c)], writes=[(HT, c)])
        norm_stats(rstd_e, "rstd_e")

    def ybuf(c):
        i0 = 2 * c if c < 4 else 16 + 2 * (c - 4)
        return big[:, i0 * 256:(i0 + 2) * 256], [("big", i0), ("big", i0 + 1)]

    def part_G1(n):
        t0 = n * T
        xT = xTs[n % 2]
        XT = "xT%d" % (n % 2)
        hT, HT, hT_reads, h_rhs = hctx(n)
        r_tg2 = Ring([0, 1, 2])
        pend = []
        for c in range(8):
            if c % 4 == 0:
                wuse(TGt[c // 4])
            b = zmm(TGt[c // 4], 0, c % 4, h_rhs, hT_reads)
            if c % 4 == 3:
                wdone(TGt[c // 4])
            tg, tg_id = r_tg2.get()
            S.op("act", lambda e, tg=tg, b=b: e.activation(out=tg[:], in_=ps[b][:], func=AF.Tanh, scale=0.5), reads=[("ps", b)], writes=[tg_id])
            if pend:
                pend.pop(0)()
            S.op("dve", lambda e, tg=tg, c=c: e.scalar_tensor_tensor(out=tg[:], in0=tg[:], scalar=1.0, in1=xc[:, c, :], op0=ALU.add, op1=ALU.mult),
                 reads=[tg_id, ("xc", c)], writes=[tg_id])
            S.op("dve", lambda e, tg=tg: e.tensor_tensor(out=tg[:], in0=tg[:], in1=rstd_e[:], op=ALU.mult), reads=[tg_id, "rstd_e"], writes=[tg_id])
            S.op("dve", lambda e, tg=tg, c=c: e.tensor_tensor(out=xT[:, c, :], in0=tg[:], in1=xT[:, c, :], op=ALU.add),
                 reads=[tg_id, (XT, c)], writes=[(XT, c)])
            pend.append(lambda c=c: S.op("act", lambda e: e.activation(out=bigbf(8 + c), in_=xT[:, c, :], func=AF.Square), reads=[(XT, c)], writes=[("big", 8 + c)]))
        while pend:
            pend.pop(0)()

    def part_G1b(n):
        xT = xTs[n % 2]
        XT = "xT%d" % (n % 2)
        norm_stats(rstd, "rstd", alt=True)
        for c in range(8):
            yb_, yids = ybuf(c)
            S.op("dve", lambda e, c=c, yb_=yb_: e.scalar_tensor_tensor(out=yb_, in0=xT[:, c, :], scalar=pcol(R_GFIN + c), in1=rstd[:], op0=ALU.mult, op1=ALU.mult),
                 reads=[(XT, c), "rstd", "prm"], writes=yids)

    def part_G2(n):
        t0 = n * T
        for tt in range(4):
            ob = xs[tt % 2]
            oid = ("xs", tt % 2)
            for g in range(2):
                b = newbank()

                def tro(e, g=g, b=b, tt=tt):
                    for q in range(4):
                        ins = e.transpose(out=ps[b][:, q * 128:(q + 1) * 128], in_=ybuf(4 * g + q)[0][:, tt * 128:(tt + 1) * 128], identity=ident[:])
                    return ins
                S.op("pe", tro, reads=[i_ for q in range(4) for i_ in ybuf(4 * g + q)[1]] + ["ident"], writes=[("ps", b)])
                if g == 0:
                    S.op("act", lambda e, ob=ob, b=b: e.activation(out=ob[:, 0:512], in_=ps[b][:], func=AF.Copy), reads=[("ps", b)], writes=[oid])
                else:
                    S.op("dve", lambda e, ob=ob, b=b: e.tensor_copy(out=ob[:, 512:1024], in_=ps[b][:]), reads=[("ps", b)], writes=[oid])
            S.dma("sp", lambda e, ob=ob, tt=tt: [e.dma_start(out=out[t0 + tt * 128:t0 + (tt + 1) * 128, :], in_=ob[:])], "xs%d" % (tt % 2), 1,
                  reads=[oid])

    c0_load(0)
    c0_load(1)
    c0_cast(0)
    part_S0a(0)
    part_S0a2(0)
    part_S0b(0)
    for _ in loop1_gen(0):
        pass
    for n in range(NCH):
        main_rest(n)
        more = n + 1 < NCH
        if more and n == 0:
            part_S0a(1)
        part_G1(n)
        if more:
            if n == 0:
                part_S0a2(1)
                part_S0b(1)
            g1 = loop1_gen(n + 1)
            for _ in range(3):
                next(g1)
        part_G1b(n)
        if more:
            for _ in range(3):
                next(g1)
        part_G2(n)
        if more:
            for _ in g1:
                pass

    S.emit(final_waits=["xs0", "xs1"])
    return nc


_NC_CACHE = {}


def kernel(x, p, g_mix, w_in, rnn_conv_w, rnn_conv_b, w_rg_a, b_rg_a, w_rg_x, b_rg_x,
           lru_lambda, sc_conv_w, w_proj_a, w_proj_b, w_out, g_ffn, w_up, ffn_conv_w,
           ffn_conv_b, w_down, w_ple_gate, w_ple_proj, g_ple, g_final):
    f = lambda a: np.ascontiguousarray(np.asarray(a, dtype=np.float32))
    x = f(x)
    p = f(p)
    B, S_TOK, _ = x.shape
    if S_TOK not in _NC_CACHE:
        _NC_CACHE[S_TOK] = build(S_TOK)
    nc = _NC_CACHE[S_TOK]
    shared = {
        "g_mix": f(g_mix)[0], "w_in": f(w_in)[0], "rnn_conv_w": f(rnn_conv_w)[0], "rnn_conv_b": f(rnn_conv_b)[0],
        "w_rg_a": f(w_rg_a)[0], "b_rg_a": f(b_rg_a)[0], "w_rg_x": f(w_rg_x)[0], "b_rg_x": f(b_rg_x)[0],
        "lru_lambda": f(lru_lambda)[0], "sc_conv_w": f(sc_conv_w)[0], "w_proj_a": f(w_proj_a)[0],
        "w_proj_b": f(w_proj_b)[0], "w_out": f(w_out)[0], "g_ffn": f(g_ffn)[0], "w_up": f(w_up)[0],
        "ffn_conv_w": f(ffn_conv_w)[0], "ffn_conv_b": f(ffn_conv_b)[0], "w_down": f(w_down)[0],
        "w_ple_gate": f(w_ple_gate)[0], "w_ple_proj": f(w_ple_proj)[0], "g_ple": f(g_ple)[0],
        "g_final": f(g_final), "ident": np.eye(128, dtype=np.float32),
    }
    in_maps = []
    for c in range(B):
        m = dict(shared)
        m["x"] = x[c]
        m["p"] = p[0, c]
        in_maps.append(m)
    res = run_bass_kernel_spmd(nc, in_maps, core_ids=list(range(B)))
    return np.stack([np.asarray(r["out"], dtype=np.float32) for r in res.results], axis=0)
```
